# Optimizing a Trainium2 kernel written in Bass

```python
import jax, jax.numpy as jnp
from jax import lax
import numpy as np

D_MODEL = 1024
BATCH = 4
SEQ = 4096
DEPTH = 2

N_MIXERS = 2
N_ATTN_LAYERS = (DEPTH + 1) // 2
N_HGRN_LAYERS = DEPTH // 2
ATTN_HEAD_DIM = 64
ATTN_HEADS = D_MODEL // ATTN_HEAD_DIM
DILATED_PATTERNS = ((128, 1), (512, 4), (2048, 16))
N_GROUPS = len(DILATED_PATTERNS)
ROPE_THETA = 10000.0
HGRN_EXPAND = 128
HGRN_HEADS = D_MODEL // HGRN_EXPAND
HGRN_DK = HGRN_EXPAND
HGRN_DV = D_MODEL // HGRN_HEADS
HGRN_CHUNK = 64
D_FF = 4 * D_MODEL
LN_EPS = 1e-5
RMS_EPS = 1e-6
DEEPNORM_ALPHA = (2 * DEPTH) ** 0.25
DEEPNORM_BETA = (8 * DEPTH) ** -0.25

kernel_name = 'hybrid_dilated_attn_hgrn2_deepnorm'

F32 = jnp.float32


def layer_norm(x, g, b):
    xf = x.astype(F32)
    mu = jnp.mean(xf, axis=-1, keepdims=True)
    var = jnp.mean(jnp.square(xf - mu), axis=-1, keepdims=True)
    return ((xf - mu) * lax.rsqrt(var + LN_EPS) * g.astype(F32) + b.astype(F32)).astype(x.dtype)


def rotary(x, pos):
    e = x.shape[-1]
    half = e // 2
    inv = ROPE_THETA ** (-jnp.arange(half, dtype=F32) * (2.0 / e))
    ang = pos.astype(F32)[:, None] * inv[None, :]
    cos = jnp.cos(ang)[None, :, None, :]
    sin = jnp.sin(ang)[None, :, None, :]
    xf = x.astype(F32)
    x1, x2 = xf[..., :half], xf[..., half:]
    return jnp.concatenate([x1 * cos - x2 * sin, x2 * cos + x1 * sin], axis=-1).astype(x.dtype)


def dilated_window_attention(q, k, v, window, dilation):
    B, S, H, E = q.shape
    blk = window // dilation
    span = dilation * blk
    s_pad = -(-S // span) * span
    n = s_pad // dilation
    nb = n // blk
    pad = ((0, 0), (0, s_pad - S), (0, 0), (0, 0))

    def to_blocks(t):
        t = jnp.pad(t, pad).reshape(B, n, dilation, H, E).transpose(0, 2, 1, 3, 4)
        return t.reshape(B * dilation, nb, blk, H, E)

    def with_prev(t):
        prev = jnp.pad(t, ((0, 0), (1, 0), (0, 0), (0, 0), (0, 0)))[:, :-1]
        return jnp.concatenate([prev, t], axis=2)

    qb, kb, vb = to_blocks(q), to_blocks(k), to_blocks(v)
    kk, vv = with_prev(kb), with_prev(vb)
    s = jnp.einsum('znqhe,znkhe->znhqk', qb, kk).astype(F32) * (E ** -0.5)
    qi = jnp.arange(blk)[:, None]
    kj = jnp.arange(2 * blk)[None, :]
    dist = qi + blk - kj
    in_band = (dist >= 0) & (dist <= blk)
    kabs = jnp.arange(nb)[:, None, None] * blk + kj[None] - blk
    valid = in_band[None] & (kabs >= 0)
    s = jnp.where(valid[None, :, None], s, -jnp.inf)
    m = jnp.max(s, axis=-1, keepdims=True)
    p = jnp.exp(s - m)
    l = jnp.sum(p, axis=-1, keepdims=True)
    o = jnp.einsum('znhqk,znkhe->znqhe', (p / l).astype(v.dtype), vv)
    lse = (m + jnp.log(l))[..., 0].transpose(0, 1, 3, 2)

    def from_blocks(t):
        t = t.reshape(B, dilation, n, *t.shape[3:]).swapaxes(1, 2)
        return t.reshape(B, s_pad, *t.shape[3:])[:, :S]

    return from_blocks(o), from_blocks(lse)


def dilated_attention_mixer(x, w_in, w_out):
    B, S, _ = x.shape
    proj = (x @ w_in).reshape(B, S, N_GROUPS, 3, ATTN_HEADS, ATTN_HEAD_DIM)
    pos = jnp.arange(S)
    outs, lses = [], []
    for g, (window, dilation) in enumerate(DILATED_PATTERNS):
        q = rotary(proj[:, :, g, 0], pos)
        k = rotary(proj[:, :, g, 1], pos)
        o, lse = dilated_window_attention(q, k, proj[:, :, g, 2], window, dilation)
        outs.append(o)
        lses.append(lse)
    wts = jax.nn.softmax(jnp.stack(lses, axis=0), axis=0)
    o = jnp.einsum('gbsh,gbshe->bshe', wts, jnp.stack(outs, axis=0).astype(F32))
    return o.reshape(B, S, ATTN_HEADS * ATTN_HEAD_DIM).astype(x.dtype) @ w_out


def forget_lower_bounds(lb_logits):
    c = jnp.cumsum(jax.nn.softmax(lb_logits.astype(F32), axis=0), axis=0)
    return c - c[0]


def hgrn2_mixer(x, w_in, w_out, norm_g, lb):
    B, S, _ = x.shape
    H, K, V, C = HGRN_HEADS, HGRN_DK, HGRN_DV, HGRN_CHUNK
    nc = S // C
    q_raw, f_raw, i_raw = jnp.split(x @ w_in, [H * K, 2 * H * K], axis=-1)
    z = f_raw.astype(F32)
    log_f = jnp.logaddexp(jnp.log(lb), jnp.log1p(-lb) + jax.nn.log_sigmoid(z))
    key = (1.0 - lb) * jax.nn.sigmoid(-z)
    q = jax.nn.silu(q_raw.astype(F32))
    v = i_raw.astype(F32)

    def chunks(t, d):
        return t.reshape(B, nc, C, H, d).transpose(0, 3, 1, 2, 4)

    q, key, log_f, v = chunks(q, K), chunks(key, K), chunks(log_f, K), chunks(v, V)
    b = jnp.cumsum(log_f, axis=3)
    q_dec = q * jnp.exp(b)
    k_dec = key * jnp.exp(-b)
    causal = jnp.tril(jnp.ones((C, C), dtype=bool))
    a = jnp.where(causal, jnp.einsum('bhcid,bhcjd->bhcij', q_dec, k_dec), 0.0)
    o_intra = jnp.einsum('bhcij,bhcje->bhcie', a, v)
    b_last = b[:, :, :, -1:, :]
    kv = jnp.einsum('bhcjd,bhcje->bhcde', key * jnp.exp(b_last - b), v)
    chunk_decay = jnp.exp(b_last[:, :, :, 0, :])

    def step(state, inp):
        dec, kv_c = inp
        return dec[..., None] * state + kv_c, state

    s0 = jnp.zeros((B, H, K, V), F32)
    _, states = lax.scan(step, s0, (jnp.moveaxis(chunk_decay, 2, 0), jnp.moveaxis(kv, 2, 0)))
    o_inter = jnp.einsum('bhcid,cbhde->bhcie', q_dec, states)
    o = (o_intra + o_inter).transpose(0, 2, 3, 1, 4).reshape(B, S, H, V)
    o = o * lax.rsqrt(jnp.mean(o * o, axis=-1, keepdims=True) + RMS_EPS) * norm_g.astype(F32).reshape(H, V)
    return o.reshape(B, S, H * V).astype(x.dtype) @ w_out


def squared_relu_mlp(x, w_up, w_down):
    return jnp.square(jax.nn.relu(x @ w_up)) @ w_down


def setup_inputs(seed: int = 0) -> dict:
    key = jax.random.key(seed)
    ks = jax.random.split(key, 13)
    d_attn_in = N_GROUPS * 3 * ATTN_HEADS * ATTN_HEAD_DIM
    d_attn_out = ATTN_HEADS * ATTN_HEAD_DIM
    d_hgrn_in = 2 * HGRN_HEADS * HGRN_DK + HGRN_HEADS * HGRN_DV
    d_hgrn_out = HGRN_HEADS * HGRN_DV
    nrm = lambda k, shape, scale: jax.random.normal(k, shape, F32) * scale
    return {
        'x': nrm(ks[0], (BATCH, SEQ, D_MODEL), 1.0),
        'attn_w_in': nrm(ks[1], (N_ATTN_LAYERS, D_MODEL, d_attn_in), D_MODEL ** -0.5),
        'attn_w_out': nrm(ks[2], (N_ATTN_LAYERS, d_attn_out, D_MODEL), d_attn_out ** -0.5 * DEEPNORM_BETA),
        'hgrn_w_in': nrm(ks[3], (N_HGRN_LAYERS, D_MODEL, d_hgrn_in), D_MODEL ** -0.5),
        'hgrn_w_out': nrm(ks[4], (N_HGRN_LAYERS, d_hgrn_out, D_MODEL), d_hgrn_out ** -0.5 * DEEPNORM_BETA),
        'hgrn_norm_g': 1.0 + nrm(ks[5], (N_HGRN_LAYERS, d_hgrn_out), 0.02),
        'lb_logits': nrm(ks[6], (DEPTH, HGRN_HEADS * HGRN_DK), 0.1),
        'ln_mix_g': 1.0 + nrm(ks[7], (DEPTH, D_MODEL), 0.02),
        'ln_mix_b': nrm(ks[8], (DEPTH, D_MODEL), 0.02),
        'ln_ffn_g': 1.0 + nrm(ks[9], (DEPTH, D_MODEL), 0.02),
        'ln_ffn_b': nrm(ks[10], (DEPTH, D_MODEL), 0.02),
        'ffn_w_up': nrm(ks[11], (DEPTH, D_MODEL, D_FF), D_MODEL ** -0.5),
        'ffn_w_down': nrm(ks[12], (DEPTH, D_FF, D_MODEL), D_FF ** -0.5 * DEEPNORM_BETA),
    }


def reference(x, attn_w_in, attn_w_out, hgrn_w_in, hgrn_w_out, hgrn_norm_g, lb_logits,
              ln_mix_g, ln_mix_b, ln_ffn_g, ln_ffn_b, ffn_w_up, ffn_w_down):
    lbs = forget_lower_bounds(lb_logits)
    for i in range(DEPTH):
        j = i // N_MIXERS
        if i % N_MIXERS == 0:
            y = dilated_attention_mixer(x, attn_w_in[j], attn_w_out[j])
        else:
            y = hgrn2_mixer(x, hgrn_w_in[j], hgrn_w_out[j], hgrn_norm_g[j], lbs[i])
        x = layer_norm(DEEPNORM_ALPHA * x + y, ln_mix_g[i], ln_mix_b[i])
        y = squared_relu_mlp(x, ffn_w_up[i], ffn_w_down[i])
        x = layer_norm(DEEPNORM_ALPHA * x + y, ln_ffn_g[i], ln_ffn_b[i])
    return x
```

```python
import contextlib
import numpy as np
import concourse.bass as bass
import concourse.mybir as mybir
from concourse.bass_utils import run_bass_kernel_spmd

F32 = mybir.dt.float32
BF16 = mybir.dt.bfloat16
AF = mybir.ActivationFunctionType
ALU = mybir.AluOpType

NT = 2048
D = 1024
ALPHA = 4.0 ** 0.25
LN_EPS = 1e-5
RMS_EPS = 1e-6
DIL = (1, 4, 16)
ENGS = ("pe", "act", "dve", "pool", "sp")
EPOCH = 20000


class Buf:
    __slots__ = ("lw", "rd", "excl")

    def __init__(self, excl=False):
        self.lw = None
        self.rd = []
        self.excl = excl


class Op:
    __slots__ = ("id", "eng", "fn", "deps", "sem", "inc", "semval", "cost", "seg", "pos", "needed", "cnt", "start", "fin", "cp", "tset")


DEF_COST = {"pe": 0.28, "act": 0.62, "dve": 0.68, "pool": 1.3, "sp": 0.1}


class _Probe:
    def __getattr__(self, name):
        def f(*a, **k):
            return (name, a, k)
        return f


def _free_elems(ap):
    try:
        n = 1
        for st_, cnt_ in list(ap.ap)[1:]:
            n *= int(cnt_)
        return n, int(list(ap.ap)[0][1])
    except Exception:
        return 512, 128


_TSET = {"Exp": "E", "Ln": "E", "Sigmoid": "S", "Silu": "L", "Sqrt": "Q"}


def _act_table(fn):
    try:
        name, a, k = fn(_Probe())
        f = k.get("func")
        if f is None:
            return None
        nm = getattr(f, "name", None) or str(f).split(".")[-1]
        return _TSET.get(nm)
    except Exception:
        return None


def _est_cost(eng, fn, is_dma):
    try:
        name, a, k = fn(_Probe())
    except Exception:
        return 3.0 if is_dma else DEF_COST[eng]
    out = k.get("out", a[0] if a else None)
    n, parts = _free_elems(out) if out is not None else (512, 128)
    if is_dma:
        if name == "collective_compute":
            return 40.0
        return 2.0 + n * parts * 4 / 300e3
    if eng == "pe":
        return 0.10 if name == "transpose" else 0.065 + n * 0.00043
    if eng == "act":
        return 0.24 + n * 0.00072
    if eng == "dve":
        c = 0.09 + n * 0.00115
        if name == "reciprocal":
            c = 0.1 + n * 0.0038
        elif name == "tensor_tensor_scan":
            c = 0.1 + n * 0.0022
        return c
    if eng == "pool":
        if name == "tensor_copy":
            return 0.15 + n * 0.0037
        return 0.15 + n * 0.0022
    return DEF_COST[eng]


class Sched:
    def __init__(self, nc):
        self.nc = nc
        self.ops = []
        self.seg = 0
        self.dma_issued = {}
        self.dma_inc = {}
        self.dma_last = {}
        self.final_waits = []
        self.cur_delta = 0.0
        self.delta_by_seg = {}

    def _mk(self, eng, fn, reads, writes, cost, is_dma=False):
        op = Op()
        op.id = len(self.ops)
        op.eng = eng
        op.fn = fn
        op.sem = None
        op.inc = 0
        op.semval = 0
        op.seg = self.seg
        self.delta_by_seg[self.seg] = self.cur_delta
        op.needed = False
        op.cnt = None
        op.cost = _est_cost(eng, fn, is_dma) if cost is None else cost
        op.tset = _act_table(fn) if (eng == "act" and not is_dma) else None
        deps = set()
        for b in reads:
            if b.lw is not None:
                deps.add(b.lw)
            if b.excl:
                for r in b.rd:
                    if self.ops[r].eng != eng or self.ops[r].sem is not None:
                        deps.add(r)
        for b in writes:
            if b.lw is not None:
                deps.add(b.lw)
            deps.update(b.rd)
        op.deps = deps
        self.ops.append(op)
        for b in reads:
            b.rd.append(op.id)
        for b in writes:
            b.lw = op.id
            b.rd = []
        return op

    def op(self, eng, fn, reads=(), writes=(), cost=None):
        return self._mk(eng, fn, reads, writes, cost)

    def dma(self, qeng, sem, fn, reads=(), writes=(), inc=16, cost=None):
        op = self._mk(qeng, fn, reads, writes, cost, is_dma=True)
        n = self.dma_issued.get(sem, 0) + 1
        self.dma_issued[sem] = n
        self.dma_inc[sem] = inc
        op.sem = sem
        op.inc = inc
        op.semval = n * inc
        if sem in self.dma_last:
            op.deps.add(self.dma_last[sem])
        self.dma_last[sem] = op.id
        return op

    def barrier(self):
        self.seg += 1

    def wait_all_dma(self, qeng, sems):
        for s in sems:
            self.final_waits.append((qeng, s, self.dma_issued[s] * self.dma_inc[s]))

    def _schedule(self, seg_ops):
        import heapq
        ops = self.ops
        ids = [o.id for o in seg_ops]
        inseg = set(ids)
        succ = {i: [] for i in ids}
        npred = {}
        for o in seg_ops:
            d = [x for x in o.deps if x in inseg]
            npred[o.id] = len(d)
            for x in d:
                succ[x].append(o.id)
        for i in reversed(ids):
            o = ops[i]
            o.cp = o.cost + max([ops[j].cp for j in succ[i]], default=0.0)
        free = {e: 0.0 for e in ENGS}
        ready_t = {i: 0.0 for i in ids}
        order = {e: [] for e in ENGS}
        ready = {e: [] for e in ENGS}
        for i in ids:
            if npred[i] == 0:
                ready[ops[i].eng].append(i)
        remaining = len(ids)
        cur_tab = [None]
        import os
        TAB = float(os.environ.get("KS_TAB", "1.3"))
        HOP = float(os.environ.get("KS_HOP", "0.25"))
        DELTA = self.delta_by_seg.get(seg_ops[0].seg, 0.0) if seg_ops else 0.0
        if "KS_DELTA" in os.environ:
            DELTA = float(os.environ["KS_DELTA"])

        def pen(i):
            o_ = ops[i]
            return TAB if (o_.tset is not None and o_.tset != cur_tab[0]) else 0.0

        while remaining:
            best = None
            for e in ENGS:
                if not ready[e]:
                    continue
                if e == "act":
                    stf = lambda i: max(free[e], ready_t[i]) + pen(i)
                else:
                    stf = lambda i: max(free[e], ready_t[i])
                if DELTA > 0:
                    m0 = min(stf(i) for i in ready[e])
                    c = min((i for i in ready[e] if stf(i) <= m0 + DELTA), key=lambda i: (-ops[i].cp, i))
                else:
                    c = min(ready[e], key=lambda i: (stf(i), -ops[i].cp, i))
                st_ = max(free[e], ready_t[c])
                if best is None or (st_, -ops[c].cp, c) < best[0]:
                    best = ((st_, -ops[c].cp, c), e, c, st_)
            _, e, c, st_ = best
            ready[e].remove(c)
            o = ops[c]
            extra = 0.0
            if e == "act" and o.tset is not None:
                extra = pen(c)
                cur_tab[0] = o.tset
            st_ += extra
            o.start = st_
            issue = 0.08 if o.sem is not None else o.cost
            free[e] = st_ + issue
            o.fin = st_ + o.cost + HOP
            order[e].append(c)
            remaining -= 1
            for j in succ[c]:
                ready_t[j] = max(ready_t[j], o.fin)
                npred[j] -= 1
                if npred[j] == 0:
                    ready[ops[j].eng].append(j)
        return order

    def emit(self, stack):
        nc = self.nc
        ops = self.ops
        nseg = self.seg + 1
        segs = [[] for _ in range(nseg)]
        for o in ops:
            segs[o.seg].append(o)
        final = {e: [] for e in ENGS}
        for si in range(nseg):
            order = self._schedule(segs[si])
            for e in ENGS:
                for k, i in enumerate(order[e]):
                    final[e].append((ops[i], si if (k == 0 and si > 0) else None))
        for e in ENGS:
            for p, (o, _) in enumerate(final[e]):
                o.pos = p
        last_in_seg = [{} for _ in range(nseg)]
        dma_in_seg = [{} for _ in range(nseg)]
        for e in ENGS:
            for (o, _) in final[e]:
                if o.sem is None:
                    last_in_seg[o.seg][e] = o
                else:
                    dma_in_seg[o.seg][o.sem] = max(dma_in_seg[o.seg].get(o.sem, 0), o.semval)
        waits = {}
        waited = {e: {} for e in ENGS}
        for e in ENGS:
            w = waited[e]
            for (o, barrier_seg) in final[e]:
                need = {}

                def add(key, val):
                    if need.get(key, -1) < val:
                        need[key] = val

                if barrier_seg is not None:
                    for ps in range(barrier_seg):
                        for e2, lo in last_in_seg[ps].items():
                            if e2 != e or e != "pe":
                                add(("eng", e2), lo.pos)
                        for sname, v in dma_in_seg[ps].items():
                            add(("dma", sname), v)
                for d in o.deps:
                    od = ops[d]
                    if od.seg != o.seg:
                        continue
                    if od.sem is not None:
                        add(("dma", od.sem), od.semval)
                    elif od.eng == e and e == "pe" and o.sem is None:
                        continue
                    else:
                        add(("eng", od.eng), od.pos)
                lst = []
                for key, val in need.items():
                    if w.get(key, -1) >= val:
                        continue
                    w[key] = val
                    lst.append((key, val))
                    if key[0] == "eng":
                        final[key[1]][val][0].needed = True
                waits[o.id] = lst
        eng_sems = {}
        for e in ENGS:
            c = 0
            for (o, _) in final[e]:
                if o.needed and o.sem is None:
                    c += 1
                    o.cnt = c
            nep = c // EPOCH + 1
            eng_sems[e] = [stack.enter_context(nc.semaphore(f"s_{e}_{i}")) for i in range(nep)]
        dma_sems = {s: stack.enter_context(nc.semaphore(f"d_{s}")) for s in self.dma_issued}

        def resolve(key, val):
            if key[0] == "dma":
                return dma_sems[key[1]], val
            c = final[key[1]][val][0].cnt
            ep, r = divmod(c - 1, EPOCH)
            return eng_sems[key[1]][ep], r + 1

        def run(e, engobj):
            for (o, _) in final[e]:
                for (key, val) in waits[o.id]:
                    s, v = resolve(key, val)
                    engobj.wait_ge(s, v)
                r = o.fn(engobj)
                if o.sem is not None:
                    r.then_inc(dma_sems[o.sem], o.inc)
                elif o.needed:
                    ep = (o.cnt - 1) // EPOCH
                    r.then_inc(eng_sems[e][ep], 1)
            for (qe, s, v) in self.final_waits:
                if qe == e:
                    engobj.wait_ge(dma_sems[s], v)

        with nc.Block() as block:
            @block.tensor
            def _(t):
                run("pe", t)

            @block.scalar
            def _(t):
                run("act", t)

            @block.vector
            def _(t):
                run("dve", t)

            @block.gpsimd
            def _(t):
                run("pool", t)

            @block.sync
            def _(t):
                run("sp", t)


def sl(start, n, step=1):
    return slice(start, start + (n - 1) * step + 1, step)


class Alloc:
    def __init__(self, A, Ab, nbytes):
        self.A = A
        self.Ab = Ab
        self.n = nbytes
        self.pos = 0

    def f32(self, n):
        self.pos = (self.pos + 63) // 64 * 64
        o = self.pos
        self.pos += n * 4
        assert self.pos <= self.n, ("arena overflow", self.pos, self.n)
        return self.A[:, o // 4:o // 4 + n]

    def bf(self, n):
        self.pos = (self.pos + 63) // 64 * 64
        o = self.pos
        n2 = (n + 1) // 2 * 2
        self.pos += n2 * 2
        assert self.pos <= self.n, ("arena overflow", self.pos, self.n)
        return self.Ab[:, o // 2:o // 2 + n]


def build(stage=99, mode="full"):
    nc = bass.Bass("TRN2", target_bir_lowering=False)

    def din(name, shape):
        return nc.dram_tensor(name, shape, F32, kind="ExternalInput")

    def dout(name, shape):
        return nc.dram_tensor(name, shape, F32, kind="ExternalOutput")

    do_l0 = mode in ("full", "l0")
    do_l1 = mode in ("full", "l1")
    t_xo = din("xo", [NT, D])
    t_ident = din("ident", [128, 128])
    t_lng = din("lng", [4, D])
    t_lnb = din("lnb", [4, D])
    t_wup = din("wup", [2, D, 4 * D])
    t_wdn = din("wdn", [2, 4 * D, D])
    if do_l0:
        t_xh = din("xh", [NT, D])
        t_cs = din("cs", [128, 4096])
        t_sn = din("sn", [128, 4096])
        t_pats = din("pats", [128, 512])
        t_perm = din("perm", [128, 128])
        t_win = din("awin", [D, 9 * D])
        t_wout = din("awout", [D, D])
    if do_l1:
        t_hwin = din("hwin", [D, 3 * D])
        t_hwout = din("hwout", [D, D])
        t_hng = din("hng", [128, 8])
        t_lbl = din("lbl", [128, 16])
        if mode == "l1":
            t_s0 = din("s0", [8, 128, 128])
            t_sout = dout("sout", [8, 128, 128])
        else:
            t_flags = din("flags", [128, 2])
            t_bin = nc.dram_tensor("sx_in", [1024, 128], F32)
            t_bout = nc.dram_tensor("sx_out", [1024, 128], F32)
        t_tri = din("tri", [128, 128])
        t_rst = din("rst", [128, 512])
    t_out = dout("out", [NT, D])

    st = contextlib.ExitStack()
    with st:
        NF = 53200
        A = st.enter_context(nc.sbuf_tensor("arena", [128, NF], F32))
        Ab = A.bitcast(BF16)
        al = Alloc(A, Ab, NF * 4)
        PSF = [st.enter_context(nc.psum_tensor(f"ps{i}", [128, 512], F32)) for i in range(8)]
        PSB = [p.bitcast(BF16) for p in PSF]
        BPS = [Buf(excl=True) for _ in range(8)]
        S = Sched(nc)
        out_sems = []

        XT = al.bf(8 * NT).rearrange("p (a b) -> p a b", a=8)
        BXT = [Buf() for _ in range(4)]
        XTOK = al.f32(16 * D).rearrange("p (a b) -> p a b", a=16)
        BXTOK = [Buf() for _ in range(16)]
        xtok_off = 8 * NT * 2
        IDENT = al.bf(128)
        BID = Buf()
        S.dma("pool", "c_id", lambda e: e.dma_start(out=IDENT, in_=t_ident.ap()), writes=[BID])
        NLN = 4
        STATS_L = [al.f32(32) for _ in range(NLN)]
        BSTATS_L = [Buf() for _ in range(NLN)]
        LNR = {}
        BGB = Buf()
        BGB2 = Buf()
        BLNT_L = [Buf() for _ in range(NLN)]
        BLNB_L = [Buf() for _ in range(NLN)]
        lncnt = {"n": 0}

        def alloc_ln(row):
            GB = al.f32(2 * D).rearrange("p (a b) -> p a b", a=2)
            LNR["GB"] = GB
            LNR["LNT"] = [al.f32(D) for _ in range(NLN)]
            LNR["LNB"] = [al.bf(D) for _ in range(NLN)]

            def bc(t):
                return bass.AP(t, row * D, [[0, 128], [1, D]])
            S.dma("sp", "c_g", lambda e: e.dma_start(out=GB[:, 0, :], in_=bc(t_lng)), writes=[BGB])
            S.dma("sp", "c_b", lambda e: e.dma_start(out=GB[:, 1, :], in_=bc(t_lnb)), writes=[BGB2])

        def emit_ln_all(final=False):
            GB = LNR["GB"]

            def bufs(t):
                k_ = t % NLN
                return LNR["LNT"][k_], LNR["LNB"][k_], STATS_L[k_], BSTATS_L[k_], BLNT_L[k_], BLNB_L[k_]

            def s1(t):
                z = XTOK[:, t, :]
                bz = BXTOK[t]
                LNT, LNB, STATS, BSTATS, BLNT, BLNB = bufs(t)
                S.op("dve", lambda e: e.bn_stats(out=STATS[:, 0:6], in_=z[:, 0:512]), reads=[bz], writes=[BSTATS])
                S.op("dve", lambda e: e.bn_stats(out=STATS[:, 6:12], in_=z[:, 512:1024]), reads=[bz], writes=[BSTATS])
                S.op("dve", lambda e: e.bn_aggr(out=STATS[:, 12:14], in_=STATS[:, 0:12]), reads=[BSTATS], writes=[BSTATS])
                S.op("act", lambda e: e.activation(out=STATS[:, 14:15], in_=STATS[:, 13:14], func=AF.Sqrt, bias=EPS_T[:, 0:1], scale=1.0),
                     reads=[BSTATS, BCONST], writes=[BSTATS])
                S.op("dve", lambda e: e.reciprocal(out=STATS[:, 15:16], in_=STATS[:, 14:15]), reads=[BSTATS], writes=[BSTATS])
                S.op("dve", lambda e: e.tensor_scalar(out=STATS[:, 16:17], in0=STATS[:, 12:13], scalar1=STATS[:, 15:16], scalar2=-1.0,
                                                      op0=ALU.mult, op1=ALU.mult), reads=[BSTATS], writes=[BSTATS])
                S.op("act", lambda e: e.activation(out=LNT, in_=z, func=AF.Identity, bias=STATS[:, 16:17], scale=STATS[:, 15:16]),
                     reads=[bz, BSTATS], writes=[BLNT])

            def s2(t):
                z = XTOK[:, t, :]
                bz = BXTOK[t]
                LNT, LNB, STATS, BSTATS, BLNT, BLNB = bufs(t)
                S.op("dve", lambda e: e.tensor_tensor(out=LNT, in0=LNT, in1=GB[:, 0, :], op=ALU.mult), reads=[BLNT, BGB], writes=[BLNT])
                S.op("pool", lambda e: e.tensor_tensor(out=z, in0=LNT, in1=GB[:, 1, :], op=ALU.add), reads=[BLNT, BGB2], writes=[bz])

            def s3(t):
                z = XTOK[:, t, :]
                bz = BXTOK[t]
                LNT, LNB, STATS, BSTATS, BLNT, BLNB = bufs(t)
                if final:
                    S.dma("sp", f"o_out{t % 4}", lambda e: e.dma_start(out=t_out.ap()[t * 128:(t + 1) * 128, :], in_=z), reads=[bz])
                    return
                S.op("act", lambda e: e.copy(out=LNB, in_=z), reads=[bz], writes=[BLNB])
                pb = (6, 7, 0, 1)[t % 4]
                for kc in range(8):
                    S.op("pe", lambda e, kc=kc: e.transpose(out=PSB[pb][:, kc * 128:(kc + 1) * 128], in_=LNB[:, kc * 128:(kc + 1) * 128], identity=IDENT),
                         reads=[BLNB, BID], writes=[BPS[pb]])

            def s4(t):
                pb = (6, 7, 0, 1)[t % 4]
                dst = XT[:, :, t * 128:(t + 1) * 128]
                src = PSB[pb][:, 0:1024].rearrange("p (a b) -> p a b", a=8)
                S.op("dve", lambda e: e.tensor_copy(out=dst, in_=src), reads=[BPS[pb]], writes=[BXT[t // 4]])

            for tau in range(16 + 3):
                if tau < 16:
                    s1(tau)
                if 0 <= tau - 1 < 16:
                    s2(tau - 1)
                if 0 <= tau - 2 < 16:
                    s3(tau - 2)
                if 0 <= tau - 3 < 16 and not final:
                    s4(tau - 3)

        EPS_T = al.f32(2)
        BCONST = Buf()
        S.op("dve", lambda e: e.memset(EPS_T[:, 0:1], LN_EPS), writes=[BCONST])
        S.op("dve", lambda e: e.memset(EPS_T[:, 1:2], RMS_EPS), writes=[BCONST])
        base_pos = al.pos

        def emit_outproj_residual(srcT, bsrc, WB, bwb, nkc):
            for t in range(16):
                for hf in range(2):
                    pb = (4, 5, 2, 3)[(2 * t + hf) % 4]
                    for kc in range(nkc):
                        S.op("pe", lambda e, kc=kc, hf=hf, pb=pb, t=t: e.matmul(PSF[pb][:, :], lhsT=srcT[:, kc, t * 128:(t + 1) * 128],
                                                                               rhs=WB[:, kc, hf * 512:(hf + 1) * 512], start=(kc == 0), stop=(kc == nkc - 1)),
                             reads=[bsrc, bwb[kc]], writes=[BPS[pb]])
                    zz = XTOK[:, t, hf * 512:(hf + 1) * 512]
                    S.op("dve", lambda e, zz=zz, pb=pb: e.scalar_tensor_tensor(out=zz, in0=zz, scalar=ALPHA, in1=PSF[pb][:, :], op0=ALU.mult, op1=ALU.add),
                         reads=[BXTOK[t], BPS[pb]], writes=[BXTOK[t]])

        def emit_ffn(layer, final):
            mark = al.pos
            HT = [al.bf(4 * NT).rearrange("p (a b) -> p a b", a=4) for _ in range(2)]
            BHT = [[Buf() for _ in range(4)] for _ in range(2)]
            WU = [al.bf(8 * 512).rearrange("p (a b) -> p a b", a=8) for _ in range(2)]
            WD = [al.bf(4 * 1024).rearrange("p (a b) -> p a b", a=4) for _ in range(2)]
            BWU = [Buf(), Buf()]
            BWD = [Buf(), Buf()]
            RL = [al.f32(512) for _ in range(2)]
            BRL = [Buf(), Buf()]
            alloc_ln(1 + 2 * layer)

            def load_w(fg):
                s = fg % 2
                su = t_wup.ap()[layer, :, fg * 512:(fg + 1) * 512].rearrange("(kc p) c -> p kc c", p=128)
                sd = t_wdn.ap()[layer, fg * 512:(fg + 1) * 512, :].rearrange("(fc p) c -> p fc c", p=128)
                S.dma("pool", f"wu{s}", lambda e: e.dma_start(out=WU[s], in_=su), writes=[BWU[s]])
                S.dma("pool", f"wd{s}", lambda e: e.dma_start(out=WD[s], in_=sd), writes=[BWD[s]])

            def up(fg):
                s = fg % 2
                n = 0
                for tb in range(4):
                    for fc in range(4):
                        pb = (0, 1, 4)[n % 3]
                        ri = n % 2
                        n += 1
                        for kc in range(8):
                            S.op("pe", lambda e, kc=kc, fc=fc, tb=tb, pb=pb: e.matmul(PSF[pb][:, :], lhsT=WU[s][:, kc, fc * 128:(fc + 1) * 128],
                                                                                   rhs=XT[:, kc, tb * 512:(tb + 1) * 512], start=(kc == 0), stop=(kc == 7)),
                                 reads=[BWU[s], BXT[tb]], writes=[BPS[pb]])
                        S.op("act", lambda e, pb=pb, ri=ri: e.activation(out=RL[ri], in_=PSF[pb][:, :], func=AF.Relu), reads=[BPS[pb]], writes=[BRL[ri]])
                        S.op("pool", lambda e, ri=ri, fc=fc, tb=tb: e.tensor_tensor(out=HT[s][:, fc, tb * 512:(tb + 1) * 512], in0=RL[ri], in1=RL[ri], op=ALU.mult),
                             reads=[BRL[ri]], writes=[BHT[s][tb]])

            def down(fg):
                s = fg % 2
                n = 0
                for t in range(16):
                    for hf in range(2):
                        pb = (2, 3, 5)[n % 3]
                        n += 1
                        for fc in range(4):
                            S.op("pe", lambda e, fc=fc, hf=hf, t=t, pb=pb: e.matmul(PSF[pb][:, :], lhsT=HT[s][:, fc, t * 128:(t + 1) * 128],
                                                                                 rhs=WD[s][:, fc, hf * 512:(hf + 1) * 512], start=(fc == 0), stop=(fc == 3)),
                                 reads=[BHT[s][t // 4], BWD[s]], writes=[BPS[pb]])
                        zz = XTOK[:, t, hf * 512:(hf + 1) * 512]
                        if fg == 0:
                            S.op("dve", lambda e, zz=zz, pb=pb: e.scalar_tensor_tensor(out=zz, in0=zz, scalar=ALPHA, in1=PSF[pb][:, :], op0=ALU.mult, op1=ALU.add),
                                 reads=[BXTOK[t], BPS[pb]], writes=[BXTOK[t]])
                        else:
                            S.op("dve", lambda e, zz=zz, pb=pb: e.tensor_tensor(out=zz, in0=zz, in1=PSF[pb][:, :], op=ALU.add),
                                 reads=[BXTOK[t], BPS[pb]], writes=[BXTOK[t]])

            load_w(0)
            load_w(1)
            up(0)
            for fg in range(8):
                if fg + 1 < 8:
                    up(fg + 1)
                down(fg)
                if fg + 2 < 8:
                    load_w(fg + 2)
            emit_ln_all(final=final)
            S.barrier()
            al.pos = mark

        def emit_attention():
            mark = al.pos
            S.cur_delta = 0.0
            XH = Ab[:, xtok_off // 2: xtok_off // 2 + 8 * NT].rearrange("p (a b) -> p a b", a=8)
            BXH = [Buf() for _ in range(4)]
            o = xtok_off + 8 * NT * 2
            CS = A[:, o // 4: o // 4 + 4096]
            o += 4096 * 4
            SN = A[:, o // 4: o // 4 + 4096]
            BTAB = Buf()
            S.dma("sp", "c_cs", lambda e: e.dma_start(out=CS, in_=t_cs.ap()), writes=[BTAB])
            S.dma("sp", "c_sn", lambda e: e.dma_start(out=SN, in_=t_sn.ap()), writes=[BTAB])
            OT = al.bf(8 * NT).rearrange("p (a b) -> p a b", a=8)
            BOT = Buf()
            mark_ot = al.pos
            OACC = [al.f32(NT) for _ in range(2)]
            BOACC = [[Buf(), Buf()] for _ in range(2)]
            QT = al.bf(NT)
            BQT = [Buf() for _ in range(4)]
            KT = al.bf(4096)
            BKT = [Buf() for _ in range(8)]
            VT = al.bf(4096)
            BVT = [Buf() for _ in range(8)]
            VAUG = al.bf(32 * 256).rearrange("p (t h c) -> p t h c", t=32, h=2)
            BVA = Buf()
            WQ = [[al.bf(8 * 128).rearrange("p (a b) -> p a b", a=8) for _ in range(3)] for _ in range(2)]
            BW = [[Buf() for _ in range(3)] for _ in range(2)]
            QB = [al.bf(512) for _ in range(2)]
            BQB = [Buf(), Buf()]
            RA = [al.f32(512) for _ in range(2)]
            BRA = [Buf(), Buf()]
            RB = [al.f32(512) for _ in range(2)]
            BRB = [Buf(), Buf()]
            NPT = 3
            PT = [al.bf(512) for _ in range(NPT)]
            BPT = [Buf() for _ in range(NPT)]
            PATS = al.bf(512)
            BPAT = Buf()
            PERM = al.bf(128)
            BPERM = Buf()
            NXIN = 4
            XIN = [OT[:, i, 0:1024] for i in range(NXIN)]
            BXIN = [Buf() for _ in range(NXIN)]
            S.dma("pool", "c_pat", lambda e: e.dma_start(out=PATS, in_=t_pats.ap()), writes=[BPAT])

            def pat_views(pat, pt):
                if pat == 1:
                    return PATS[:, 0:512], pt
                if pat == 0:
                    m = PATS[:, 256:512]
                    return bass.AP(m.tensor, m.offset, [m.ap[0], [0, 2], [1, 256]]), pt.rearrange("p (a b) -> p a b", a=2)
                m = PATS[:, (pat - 2) * 64:(pat - 2) * 64 + 64]
                return (bass.AP(m.tensor, m.offset, [m.ap[0], [0, 4], [128, 2], [1, 64]]),
                        pt.rearrange("p (a b c) -> p a b c", a=4, b=2))
            S.dma("pool", "c_perm", lambda e: e.dma_start(out=PERM, in_=t_perm.ap()), writes=[BPERM])
            S.op("pool", lambda e: e.memset(VAUG[:, :, :, 64:128], 1.0), writes=[BVA])

            for ti, t in enumerate(list(range(16, 32)) + list(range(15, -1, -1))):
                srcd = (t_xh if t < 16 else t_xo).ap()[(t % 16) * 128:(t % 16 + 1) * 128, :]
                s = ti % NXIN
                S.dma("pool", f"xin{s}", lambda e, s=s, srcd=srcd: e.dma_start(out=XIN[s], in_=srcd), writes=[BXIN[s]])
                pb = 6 + (t % 2)
                for kc in range(8):
                    S.op("pe", lambda e, kc=kc, s=s, pb=pb: e.transpose(out=PSB[pb][:, kc * 128:(kc + 1) * 128], in_=XIN[s][:, kc * 128:(kc + 1) * 128], identity=IDENT),
                         reads=[BXIN[s], BID], writes=[BPS[pb]])
                tt = t % 16
                dst = (XH if t < 16 else XT)[:, :, tt * 128:(tt + 1) * 128]
                bd = (BXH if t < 16 else BXT)[tt // 4]
                src = PSB[pb][:, 0:1024].rearrange("p (a b) -> p a b", a=8)
                eng = "act" if t % 2 else "dve"
                if eng == "act":
                    S.op("act", lambda e, dst=dst, src=src: e.copy(out=dst, in_=src), reads=[BPS[pb]], writes=[bd])
                else:
                    S.op("dve", lambda e, dst=dst, src=src: e.tensor_copy(out=dst, in_=src), reads=[BPS[pb]], writes=[bd])

            def xcols(kc, c0, n):
                if c0 < 2048:
                    return XH[:, kc, c0:c0 + n], BXH[c0 // 512]
                return XT[:, kc, c0 - 2048:c0 - 2048 + n], BXT[(c0 - 2048) // 512]

            def load_w(it):
                hp, g = divmod(it, 3)
                s = it % 2
                for wh in range(3):
                    c0 = g * 3072 + wh * 1024 + hp * 128
                    src = t_win.ap()[:, c0:c0 + 128].rearrange("(kc p) c -> p kc c", p=128)
                    S.dma("pool", f"wq{s}{wh}", lambda e, s=s, wh=wh, src=src: e.dma_start(out=WQ[s][wh], in_=src), writes=[BW[s][wh]])

            cnt = {"pj": 0, "st": 0}
            import os
            ROT_ADD_ENG = "dve" if "addondve" in os.environ.get("KDBG", "") else "pool"

            def proj_blocks(g, wh):
                d = DIL[g]
                if wh == 0:
                    return [(2048 + 512 * i, 512) for i in range(4)]
                c = 2048 - 128 * d
                out = []
                if c % 512:
                    out.append((c, 128))
                    c += 128
                while c < 4096:
                    out.append((c, 512))
                    c += 512
                return out

            def cm_views(T, g, wh, c0, n):
                d = DIL[g]
                if wh == 0:
                    u0, W = c0 - 2048, 2048 // d
                else:
                    u0, W = c0 - (2048 - 128 * d), (16 // d + 1) * 128
                if d == 1:
                    return T[:, u0:u0 + n], (lambda a: a)
                base = T[:, u0 // d:u0 // d + 1]
                dst = bass.AP(base.tensor, base.offset, [base.ap[0], [W, d], [1, n // d]])
                return dst, (lambda a: a.rearrange("p (i r) -> p r i", r=d))

            def project(it, wh):
                hp, g = divmod(it, 3)
                s = it % 2
                for (c0, n) in proj_blocks(g, wh):
                    pb = cnt["pj"] % 2
                    cnt["pj"] += 1
                    for kc in range(8):
                        xa, xb_ = xcols(kc, c0, n)
                        S.op("pe", lambda e, kc=kc, xa=xa, pb=pb, n=n: e.matmul(PSF[pb][:, 0:n], lhsT=WQ[s][wh][:, kc, :], rhs=xa, start=(kc == 0), stop=(kc == 7)),
                             reads=[BW[s][wh], xb_], writes=[BPS[pb]])
                    if wh == 2:
                        vdst, vview = cm_views(VT, g, 2, c0, n)
                        S.op("act", lambda e, pb=pb, n=n, vdst=vdst, vview=vview: e.copy(out=vdst, in_=vview(PSF[pb][:, 0:n])), reads=[BPS[pb]], writes=[BVT[0]])
                        continue
                    S.op("act", lambda e, pb=pb, n=n: e.copy(out=QB[pb][:, 0:n], in_=PSF[pb][:, 0:n]), reads=[BPS[pb]], writes=[BQB[pb]])
                    DBGK = os.environ.get("KDBG", "")
                    if "rot_stop1" in DBGK:
                        continue
                    if "rot_noperm" not in DBGK:
                        S.op("pe", lambda e, pb=pb, n=n: e.matmul(PSF[2][:, 0:n], lhsT=PERM, rhs=QB[pb][:, 0:n], start=True, stop=True),
                             reads=[BQB[pb], BPERM], writes=[BPS[2]])
                    if "rot_nomul" in DBGK:
                        S.op("dve", lambda e, pb=pb, c0=c0, n=n: e.tensor_copy(out=RA[pb][:, 0:n], in_=PSF[pb][:, 0:n]),
                             reads=[BPS[pb], BQB[pb]] + ([b for bb in BW for b in bb] if "waitw" in DBGK else []), writes=[BRA[pb]])
                        S.op("dve", lambda e, pb=pb, c0=c0, n=n: e.tensor_copy(out=RB[pb][:, 0:n], in_=PSF[2 if "rot_noperm" not in DBGK else pb][:, 0:n]),
                             reads=[BPS[2]], writes=[BRB[pb]])
                    else:
                        S.op("dve", lambda e, pb=pb, c0=c0, n=n: e.tensor_tensor(out=RA[pb][:, 0:n], in0=PSF[pb][:, 0:n], in1=CS[:, c0:c0 + n], op=ALU.mult),
                             reads=[BPS[pb], BTAB], writes=[BRA[pb]])
                        S.op("dve", lambda e, pb=pb, c0=c0, n=n: e.tensor_tensor(out=RB[pb][:, 0:n], in0=PSF[2 if "rot_noperm" not in DBGK else pb][:, 0:n], in1=SN[:, c0:c0 + n], op=ALU.mult),
                             reads=[BPS[2], BTAB], writes=[BRB[pb]])
                    if "rot_stop2" in DBGK:
                        continue
                    if wh == 0:
                        dst, vw = cm_views(QT, g, 0, c0, n)
                        bd = BQT[0]
                    else:
                        dst, vw = cm_views(KT, g, 1, c0, n)
                        bd = BKT[0]
                    S.op(ROT_ADD_ENG, lambda e, pb=pb, dst=dst, n=n, vw=vw: e.tensor_tensor(out=dst, in0=vw(RA[pb][:, 0:n]), in1=vw(RB[pb][:, 0:n]), op=ALU.add),
                         reads=[BRA[pb], BRB[pb]], writes=[bd])

            def build_vaug(g):
                d = DIL[g]
                nb = 16 // d
                tiles = [(r, j) for r in range(d) for j in range(nb + 1)]
                for i0 in range(0, len(tiles), 4):
                    grp = tiles[i0:i0 + 4]
                    pb = 6 + (i0 // 4) % 2
                    for k, (r, j) in enumerate(grp):
                        vt_ = r * (nb + 1) + j
                        src = VT[:, vt_ * 128:(vt_ + 1) * 128]
                        S.op("pe", lambda e, k=k, src=src, pb=pb: e.transpose(out=PSB[pb][:, k * 128:(k + 1) * 128], in_=src, identity=IDENT),
                             reads=BVT + [BID], writes=[BPS[pb]])
                    n = len(grp)
                    vt0 = i0
                    dst = VAUG[:, vt0:vt0 + n, :, 0:64]
                    src = PSB[pb][:, 0:n * 128].rearrange("p (t h c) -> p t h c", t=n, h=2)
                    S.op("dve", lambda e, dst=dst, src=src: e.tensor_copy(out=dst, in_=src), reads=[BPS[pb]], writes=[BVA])

            def qchunks(g, half):
                d = DIL[g]
                out = []
                if g == 0:
                    for b in range(1 + 8 * half, 9 + 8 * half):
                        tc = (b - 1) * 128 - half * 1024
                        out.append((0, b, 0, 128, [(tc // 512, tc % 512, 1, 128, 0)]))
                elif g == 1:
                    for r in range(4):
                        for b in (1 + 2 * half, 2 + 2 * half):
                            cmc = 256 * r + ((b - 1) % 2) * 128
                            out.append((r, b, 0, 128, [(cmc // 512, cmc % 512, 1, 128, 0)]))
                else:
                    for r in range(16):
                        cmc = 64 * r
                        out.append((r, 1, 64 * half, 64, [(cmc // 512, cmc % 512, 1, 64, 0)]))
                return out

            def attention_units(it, h):
                hp, g = divmod(it, 3)
                d = DIL[g]
                nb = 16 // d
                hs = slice(h * 64, (h + 1) * 64)
                units = []
                for half in range(2):
                    qc = qchunks(g, half)
                    per = 2 if g < 2 else 4
                    nun = len(qc) // per
                    for ui, u0 in enumerate(range(0, len(qc), per)):
                        unit = qc[u0:u0 + per]
                        if g == 2:
                            pat = 2 + half
                        elif g == 1:
                            pat = 1 if half == 0 else 0
                        else:
                            pat = 1 if (half == 0 and u0 == 0) else 0
                        mm = []
                        col = 0
                        for (r, b, i0, nq, pieces) in unit:
                            qs0 = r * (2048 // d) + (b - 1) * 128 + i0
                            qap = QT[hs, qs0:qs0 + nq]
                            for jj in (b - 1, b):
                                vt_ = r * (nb + 1) + jj
                                kap = KT[hs, vt_ * 128:(vt_ + 1) * 128]
                                mm.append((kap, qap, col, nq, r * (nb + 1) + jj, jj == b - 1, pieces))
                                col += nq
                        units.append(dict(h=h, half=half, g=g, pat=pat, mm=mm, ncol=col, last=(ui == nun - 1)))
                return units

            def emit_st(u, i):
                sb_ = (3, 7)[i % 2]
                ptb = i % NPT
                u["sb"] = sb_
                u["ptb"] = ptb
                mm = u["mm"]
                k = 0
                while k < len(mm):
                    (kap, qap, col, nq, vt, isprev, pieces) = mm[k]
                    if k + 1 < len(mm) and mm[k + 1][4] == vt and (not isprev) and mm[k + 1][5] and u["g"] < 2:
                        q2 = bass.AP(qap.tensor, qap.offset, [qap.ap[0], [1, 2 * nq]])
                        S.op("pe", lambda e, kap=kap, q2=q2, col=col, nq=nq, sb_=sb_: e.matmul(PSF[sb_][:, col:col + 2 * nq], lhsT=kap, rhs=q2, start=True, stop=True),
                             reads=BKT + BQT, writes=[BPS[sb_]])
                        k += 2
                        continue
                    S.op("pe", lambda e, kap=kap, qap=qap, col=col, nq=nq, sb_=sb_: e.matmul(PSF[sb_][:, col:col + nq], lhsT=kap, rhs=qap, start=True, stop=True),
                         reads=BKT + BQT, writes=[BPS[sb_]])
                    k += 1
                col = u["ncol"]
                pat = u["pat"]
                S.op("act", lambda e, col=col, ptb=ptb, sb_=sb_: e.activation(out=PT[ptb][:, 0:col], in_=PSF[sb_][:, 0:col], func=AF.Exp, scale=0.125),
                     reads=[BPS[sb_]], writes=[BPT[ptb]])
                assert col == 512
                mview, pview = pat_views(pat, PT[ptb])
                S.op("dve", lambda e, mview=mview, pview=pview: e.tensor_tensor(out=pview, in0=pview, in1=mview, op=ALU.mult),
                     reads=[BPT[ptb], BPAT], writes=[BPT[ptb]])

            def emit_pv(u):
                ob = (4, 5)
                h = u["h"]
                half = u["half"]
                ptb = u["ptb"]
                for (kap, qap, c0, nq, vt, isprev, pieces) in u["mm"]:
                    for (bank, pc0, pstr, pn, roff) in pieces:
                        oap = PSF[ob[bank]][:, sl(pc0, pn, pstr)]
                        S.op("pe", lambda e, oap=oap, vt=vt, ptb=ptb, c0=c0, roff=roff, pn=pn, isprev=isprev, h=h: e.matmul(
                            oap, lhsT=VAUG[:, vt, h, :], rhs=PT[ptb][:, c0 + roff:c0 + roff + pn], start=isprev, stop=(not isprev)),
                            reads=[BVA, BPT[ptb]], writes=[BPS[ob[bank]]])
                if u["last"]:
                    for bank in range(2):
                        dst = OACC[h][:, half * 1024 + bank * 512: half * 1024 + (bank + 1) * 512]
                        if u["g"] == 0:
                            S.op("act", lambda e, dst=dst, bank=bank: e.copy(out=dst, in_=PSF[ob[bank]][:, :]), reads=[BPS[ob[bank]]], writes=[BOACC[h][half]])
                        else:
                            ncls = 2 if u["g"] == 1 else 8
                            dd = DIL[u["g"]]
                            o0 = OACC[h][:, half * 1024 + ncls * bank:half * 1024 + ncls * bank + 1]
                            dview = bass.AP(o0.tensor, o0.offset, [o0.ap[0], [1, ncls], [dd, 512 // ncls]])
                            sview = PSF[ob[bank]][:, :].rearrange("p (r i) -> p r i", r=ncls)
                            S.op("dve", lambda e, dview=dview, sview=sview: e.tensor_tensor(out=dview, in0=dview, in1=sview, op=ALU.add),
                                 reads=[BPS[ob[bank]], BOACC[h][half]], writes=[BOACC[h][half]])

            def attention_it(it):
                U = attention_units(it, 0) + attention_units(it, 1)
                emit_st(U[0], 0)
                for i in range(len(U)):
                    if i + 1 < len(U):
                        emit_st(U[i + 1], i + 1)
                    emit_pv(U[i])

            def normalize(hp):
                for h in range(2):
                    for qq in range(4):
                        cs_ = slice(qq * 512, (qq + 1) * 512)
                        S.op("act", lambda e, h=h, cs_=cs_: e.activation(out=PSF[6][64:128, :], in_=OACC[h][64:128, cs_], func=AF.Ln), reads=BOACC[h], writes=[BPS[6]])
                        S.op("act", lambda e: e.activation(out=PSF[6][64:128, :], in_=PSF[6][64:128, :], func=AF.Exp, scale=-1.0), reads=[BPS[6]], writes=[BPS[6]])
                        S.op("dve", lambda e, h=h, cs_=cs_: e.tensor_tensor(out=OT[h * 64:(h + 1) * 64, hp, cs_], in0=OACC[h][0:64, cs_], in1=PSF[6][64:128, :], op=ALU.mult),
                             reads=BOACC[h] + [BPS[6]], writes=[BOT])

            import os
            NIT = int(os.environ.get("KDBG_NIT", "24"))
            DBG = os.environ.get("KDBG", "")
            if NIT < 24:
                S.op("pool", lambda e: e.memset(OT, 0.0), writes=[BOT])
                for h in range(2):
                    S.op("pool", lambda e, h=h: e.memset(OACC[h], 1.0), writes=BOACC[h])
            if NIT > 0:
                load_w(0)
                load_w(1)
            for it in range(NIT):
                hp, g = divmod(it, 3)
                for wh in range(3):
                    if "noproj%d" % wh in DBG:
                        continue
                    project(it, wh)
                if it + 2 < NIT:
                    load_w(it + 2)
                if "novaug" not in DBG:
                    build_vaug(g)
                if "noatt" not in DBG:
                    attention_it(it)
                if g == 2 and "nonorm" not in DBG:
                    normalize(hp)
            S.barrier()
            al.pos = mark_ot
            WOB = al.bf(8 * 1024).rearrange("p (a b) -> p a b", a=8)
            BWOB = [Buf() for _ in range(8)]
            for kc in range(8):
                S.dma("pool", f"c_wo{kc % 4}", lambda e, kc=kc: e.dma_start(out=WOB[:, kc, :], in_=t_wout.ap()[kc * 128:(kc + 1) * 128, :]), writes=[BWOB[kc]])
            alloc_ln(0)
            for t in range(16):
                S.dma("sp", f"xres{t % 4}", lambda e, t=t: e.dma_start(out=XTOK[:, t, :], in_=t_xo.ap()[t * 128:(t + 1) * 128, :]), writes=[BXTOK[t]])
            emit_outproj_residual(OT, BOT, WOB, BWOB, 8)
            emit_ln_all()
            S.barrier()
            al.pos = mark


        def emit_load_x():
            mark = al.pos
            XIN = al.bf(1024)
            BXIN = Buf()
            for t in range(16):
                S.dma("sp", f"xres{t % 4}", lambda e, t=t: e.dma_start(out=XTOK[:, t, :], in_=t_xo.ap()[t * 128:(t + 1) * 128, :]), writes=[BXTOK[t]])
            for t in range(16):
                S.op("act", lambda e, t=t: e.copy(out=XIN, in_=XTOK[:, t, :]), reads=[BXTOK[t]], writes=[BXIN])
                pb = 6 + (t % 2)
                for kc in range(8):
                    S.op("pe", lambda e, kc=kc, pb=pb: e.transpose(out=PSB[pb][:, kc * 128:(kc + 1) * 128], in_=XIN[:, kc * 128:(kc + 1) * 128], identity=IDENT),
                         reads=[BXIN, BID], writes=[BPS[pb]])
                dst = XT[:, :, t * 128:(t + 1) * 128]
                src = PSB[pb][:, 0:1024].rearrange("p (a b) -> p a b", a=8)
                S.op("dve", lambda e, dst=dst, src=src: e.tensor_copy(out=dst, in_=src), reads=[BPS[pb]], writes=[BXT[t // 4]])
            S.barrier()
            al.pos = mark

        def emit_hgrn(final_scan_only=False, s0_mode="dram"):
            so = final_scan_only
            S.cur_delta = 0.0
            mark = al.pos
            ON = al.bf(8 * NT).rearrange("p (a b) -> p a b", a=8)
            BON = Buf()
            mark_on = al.pos
            QD = al.bf(NT)
            KD = al.bf(NT)
            BQD = [Buf() for _ in range(4)]
            BKD = [Buf() for _ in range(4)]
            VTOK = [al.bf(16 * 128).rearrange("p (a b) -> p a b", a=16) for _ in range(2)]
            BVTOK = [[Buf() for _ in range(4)] for _ in range(2)]
            KLT = [al.bf(16 * 128).rearrange("p (a b) -> p a b", a=16) for _ in range(2)]
            BKLT = [[Buf() for _ in range(4)] for _ in range(2)]
            CD = [al.f32(32) for _ in range(2)]
            BCD = [[Buf() for _ in range(4)] for _ in range(2)]
            SALL = al.bf(33 * 128).rearrange("p (a b) -> p a b", a=33)
            BSALL = Buf()
            SF = [al.f32(128) for _ in range(2)]
            BSF = [Buf(), Buf()]
            WH = [[al.bf(8 * 128).rearrange("p (a b) -> p a b", a=8) for _ in range(3)] for _ in range(2)]
            BWH = [[Buf() for _ in range(3)] for _ in range(2)]
            T0 = [al.f32(512) for _ in range(3)]
            BT0 = [Buf() for _ in range(3)]
            T1 = [al.f32(512) for _ in range(2)]
            BT1 = [Buf() for _ in range(2)]
            T3 = [al.f32(512) for _ in range(2)]
            BT3 = [Buf() for _ in range(2)]
            if not so:
                T4 = [al.f32(512) for _ in range(2)]
                BT4 = [Buf() for _ in range(2)]
                _t5 = al.f32(512)
                T5 = [_t5, _t5]
                _bt5 = Buf()
                BT5 = [_bt5, _bt5]
            VTB = [al.bf(512) for _ in range(2)]
            BVTB = [Buf() for _ in range(2)]
            KVB = [al.f32(128) for _ in range(4)]
            BKVB = [Buf() for _ in range(4)]
            AT = al.bf(512)
            BAT = Buf()
            OS = al.f32(512)
            BOS = Buf()
            OSQ = al.bf(512)
            BOSQ = Buf()
            RS = al.f32(512)
            BRS = Buf()
            RST = al.f32(512)
            TRI = al.f32(128)
            ONES = al.bf(128)
            LBS = al.f32(40)
            BC2 = Buf()
            S.dma("sp", "c_rst", lambda e: e.dma_start(out=RST, in_=t_rst.ap()), writes=[BC2])
            S.dma("sp", "c_tri", lambda e: e.dma_start(out=TRI, in_=t_tri.ap()), writes=[BC2])
            S.dma("sp", "c_lbl", lambda e: e.dma_start(out=LBS[:, 0:16], in_=t_lbl.ap()), writes=[BC2])
            S.dma("sp", "c_hng", lambda e: e.dma_start(out=LBS[:, 32:40], in_=t_hng.ap()), writes=[BC2])
            FLG = al.f32(2)
            if mode != "l1":
                S.dma("sp", "c_flg", lambda e: e.dma_start(out=FLG, in_=t_flags.ap()), writes=[BC2])
            S.op("pool", lambda e: e.memset(ONES, 1.0), writes=[BC2])
            S.op("dve", lambda e: e.tensor_tensor(out=LBS[:, 16:24], in0=LBS[:, 8:16], in1=LBS[:, 0:8], op=ALU.subtract), reads=[BC2], writes=[BC2])
            S.op("act", lambda e: e.activation(out=LBS[:, 16:24], in_=LBS[:, 16:24], func=AF.Sigmoid), reads=[BC2], writes=[BC2])
            S.op("dve", lambda e: e.tensor_scalar(out=LBS[:, 24:32], in0=LBS[:, 16:24], scalar1=-1.0, scalar2=1.0, op0=ALU.mult, op1=ALU.add), reads=[BC2], writes=[BC2])

            def load_w(h):
                s_ = h % 2
                for wh in range(3):
                    if so and wh == 0:
                        continue
                    c0 = wh * 1024 + h * 128
                    src = t_hwin.ap()[:, c0:c0 + 128].rearrange("(kc p) c -> p kc c", p=128)
                    S.dma("pool", f"wh{s_}{wh}", lambda e, s_=s_, wh=wh, src=src: e.dma_start(out=WH[s_][wh], in_=src), writes=[BWH[s_][wh]])

            cnt = {"pj": 0}

            PJB = [0, 1, 2, 6, 7] if so else [0, 1, 5]

            def proj(h, tb, wh):
                s_ = h % 2
                pb = PJB[cnt["pj"] % len(PJB)]
                cnt["pj"] += 1
                cols = slice(tb * 512, (tb + 1) * 512)
                for kc in range(8):
                    S.op("pe", lambda e, kc=kc, pb=pb, cols=cols: e.matmul(PSF[pb][:, :], lhsT=WH[s_][wh][:, kc, :], rhs=XT[:, kc, cols], start=(kc == 0), stop=(kc == 7)),
                         reads=[BWH[s_][wh], BXT[tb]], writes=[BPS[pb]])
                return pb

            def st_A(j):
                h, tb = divmod(j, 4)
                pz = proj(h, tb, 1)
                S.op("act", lambda e: e.activation(out=T0[j % 3], in_=PSF[pz][:, :], func=AF.Sigmoid), reads=[BPS[pz]], writes=[BT0[j % 3]])
                pv = proj(h, tb, 2)
                S.op("dve", lambda e: e.tensor_copy(out=VTB[j % 2], in_=PSF[pv][:, :]), reads=[BPS[pv]], writes=[BVTB[j % 2]])
                lb = LBS[:, 16 + h:17 + h]
                oml = LBS[:, 24 + h:25 + h]
                S.op("dve", lambda e: e.tensor_scalar(out=T0[j % 3], in0=T0[j % 3], scalar1=oml, scalar2=lb, op0=ALU.mult, op1=ALU.add), reads=[BT0[j % 3], BC2], writes=[BT0[j % 3]])

            def st_Hv(j):
                h, tb = divmod(j, 4)
                for k in range(4):
                    S.op("pe", lambda e, k=k: e.transpose(out=PSB[3][:, k * 128:(k + 1) * 128], in_=VTB[j % 2][:, k * 128:(k + 1) * 128], identity=IDENT),
                         reads=[BVTB[j % 2], BID], writes=[BPS[3]])
                S.op("dve", lambda e: e.tensor_copy(out=VTOK[h % 2][:, tb * 4:(tb + 1) * 4, :], in_=PSB[3][:, 0:512].rearrange("p (a b) -> p a b", a=4)), reads=[BPS[3]], writes=[BVTOK[h % 2][tb]])

            def st_D(j):
                S.op("act", lambda e: e.activation(out=T1[j % 2], in_=T0[j % 3], func=AF.Ln), reads=[BT0[j % 3]], writes=[BT1[j % 2]])
                S.op("dve", lambda e: e.tensor_tensor_scan(out=T1[j % 2], data0=RST, data1=T1[j % 2], initial=0.0, op0=ALU.mult, op1=ALU.add),
                     reads=[BT1[j % 2], BC2], writes=[BT1[j % 2]])

            def st_F(j):
                h, tb = divmod(j, 4)
                S.op("act", lambda e: e.activation(out=CD[h % 2][:, tb * 8:(tb + 1) * 8], in_=T1[j % 2][:, 63:512:64], func=AF.Exp), reads=[BT1[j % 2]], writes=[BCD[h % 2][tb]])
                S.op("act", lambda e: e.activation(out=T3[j % 2], in_=T1[j % 2], func=AF.Exp, scale=-1.0), reads=[BT1[j % 2]], writes=[BT3[j % 2]])
                if not so:
                    S.op("act", lambda e: e.activation(out=T4[j % 2], in_=T1[j % 2], func=AF.Exp), reads=[BT1[j % 2]], writes=[BT4[j % 2]])

            def st_Aq(j):
                h, tb = divmod(j, 4)
                pq = proj(h, tb, 0)
                S.op("act", lambda e: e.activation(out=T5[j % 2], in_=PSF[pq][:, :], func=AF.Sigmoid), reads=[BPS[pq]], writes=[BT5[j % 2]])
                S.op("dve", lambda e: e.tensor_tensor(out=T5[j % 2], in0=PSF[pq][:, :], in1=T5[j % 2], op=ALU.mult), reads=[BPS[pq], BT5[j % 2]], writes=[BT5[j % 2]])

            def st_G(j):
                h, tb = divmod(j, 4)
                cols = slice(tb * 512, (tb + 1) * 512)
                S.op("dve", lambda e: e.tensor_tensor(out=T0[j % 3], in0=T0[j % 3], in1=T3[j % 2], op=ALU.mult), reads=[BT0[j % 3], BT3[j % 2]], writes=[BT0[j % 3]])
                S.op("dve", lambda e: e.tensor_tensor(out=T3[j % 2], in0=T3[j % 2], in1=T0[j % 3], op=ALU.subtract), reads=[BT0[j % 3], BT3[j % 2]], writes=[BT3[j % 2]])
                S.op("pool", lambda e: e.tensor_copy(out=KD[:, cols], in_=T3[j % 2]), reads=[BT3[j % 2]], writes=[BKD[tb]])
                if not so:
                    S.op("dve", lambda e: e.tensor_tensor(out=QD[:, cols], in0=T5[j % 2], in1=T4[j % 2], op=ALU.mult), reads=[BT5[j % 2], BT4[j % 2]], writes=[BQD[tb]])

            def st_Hk(j):
                h, tb = divmod(j, 4)
                for k in range(4):
                    S.op("pe", lambda e, k=k: e.transpose(out=PSB[3][:, k * 128:(k + 1) * 128], in_=KD[:, tb * 512 + k * 128:tb * 512 + (k + 1) * 128], identity=IDENT),
                         reads=[BKD[tb], BID], writes=[BPS[3]])
                S.op("dve", lambda e: e.tensor_copy(out=KLT[h % 2][:, tb * 4:(tb + 1) * 4, :], in_=PSB[3][:, 0:512].rearrange("p (a b) -> p a b", a=4)), reads=[BPS[3]], writes=[BKLT[h % 2][tb]])

            def head_tail(h):
                hp_ = h % 2
                if s0_mode == "dram":
                    S.dma("pool", "s0b", lambda e: e.dma_start(out=SALL[:, 0, :], in_=t_s0.ap()[h]), writes=[BSALL])
                    S.dma("sp", "s0f", lambda e: e.dma_start(out=SF[0], in_=t_s0.ap()[h]), writes=[BSF[0]])
                elif s0_mode == "zero":
                    S.op("pool", lambda e: e.memset(SF[0], 0.0), writes=[BSF[0]])
                else:
                    S.dma("sp", "s0f", lambda e: e.dma_start(out=SF[0], in_=t_bout.ap()[h * 128:(h + 1) * 128, :]), reads=[BBOUT], writes=[BSF[0]])
                    S.op("dve", lambda e: e.tensor_scalar(out=SF[0], in0=SF[0], scalar1=FLG[:, 1:2], scalar2=None, op0=ALU.mult), reads=[BSF[0], BC2], writes=[BSF[0]])
                    S.op("act", lambda e: e.copy(out=SALL[:, 0, :], in_=SF[0]), reads=[BSF[0]], writes=[BSALL])

            def scan_piece(h, kq):
                hp_ = h % 2
                for c in range(8 * kq, 8 * kq + 8):
                    t, hc = divmod(c, 2)
                    rows = slice(hc * 64, (hc + 1) * 64)
                    kb = (4 + (c // 4) % 2) if so else 4
                    kc_ = c % 4
                    S.op("pe", lambda e, t=t, rows=rows, kb=kb, kc_=kc_: e.matmul(PSF[kb][:, kc_ * 128:(kc_ + 1) * 128], lhsT=KLT[hp_][rows, t, :], rhs=VTOK[hp_][rows, t, :], start=True, stop=True),
                         reads=BKLT[hp_] + BVTOK[hp_], writes=[BPS[kb]])
                    kv = KVB[c % 4]
                    S.op("act", lambda e, kv=kv, c=c, kb=kb, kc_=kc_: e.activation(out=kv, in_=PSF[kb][:, kc_ * 128:(kc_ + 1) * 128], func=AF.Identity, scale=CD[hp_][:, c:c + 1]),
                         reads=[BPS[kb]] + BCD[hp_], writes=[BKVB[c % 4]])
                    so_, sn_ = SF[c % 2], SF[(c + 1) % 2]
                    S.op("dve", lambda e, so_=so_, sn_=sn_, c=c, kv=kv: e.scalar_tensor_tensor(out=sn_, in0=so_, scalar=CD[hp_][:, c:c + 1], in1=kv, op0=ALU.mult, op1=ALU.add),
                         reads=[BSF[c % 2], BKVB[c % 4]] + BCD[hp_], writes=[BSF[(c + 1) % 2]])
                    if not so:
                        S.op("pool", lambda e, sn_=sn_, c=c: e.tensor_copy(out=SALL[:, c + 1, :], in_=sn_), reads=[BSF[(c + 1) % 2]], writes=[BSALL])
                if kq < 3:
                    return
                if mode == "l1":
                    S.dma("sp", "sout", lambda e: e.dma_start(out=t_sout.ap()[h], in_=SF[0]), reads=[BSF[0]])
                if so:
                    S.op("act", lambda e: e.activation(out=OS[:, 0:128], in_=SF[0], func=AF.Identity, scale=FLG[:, 0:1]), reads=[BSF[0], BC2], writes=[BOS])
                    S.dma("sp", "bin", lambda e: e.dma_start(out=t_bin.ap()[h * 128:(h + 1) * 128, :], in_=OS[:, 0:128]), reads=[BOS], writes=[BBIN])

            def out_block(h, tb):
                hp_ = h % 2
                if True:
                    cols = slice(tb * 512, (tb + 1) * 512)
                    for k in range(4):
                        t = tb * 4 + k
                        tc_ = slice(t * 128, (t + 1) * 128)
                        S.op("pe", lambda e, k=k, tc_=tc_: e.matmul(PSF[2][:, k * 128:(k + 1) * 128], lhsT=KD[:, tc_], rhs=QD[:, tc_], start=True, stop=True),
                             reads=[BKD[tb], BQD[tb]], writes=[BPS[2]])
                    S.op("dve", lambda e: e.tensor_tensor(out=AT.rearrange("p (a b) -> p a b", a=4), in0=PSF[2][:, :].rearrange("p (a b) -> p a b", a=4),
                                                          in1=bass.AP(TRI.tensor, TRI.offset, [TRI.ap[0], [0, 4], [1, 128]]), op=ALU.mult),
                         reads=[BPS[2], BC2], writes=[BAT])
                    for k in range(4):
                        t = tb * 4 + k
                        S.op("pe", lambda e, k=k, t=t: e.matmul(PSF[6][:, k * 128:(k + 1) * 128], lhsT=VTOK[hp_][:, t, :], rhs=AT[:, k * 128:(k + 1) * 128], start=True, stop=False),
                             reads=BVTOK[hp_] + [BAT], writes=[BPS[6]])
                        for hc in range(2):
                            c = 2 * t + hc
                            S.op("pe", lambda e, k=k, hc=hc, c=c: e.matmul(PSF[6][:, k * 128 + hc * 64:k * 128 + (hc + 1) * 64], lhsT=SALL[:, c, :],
                                                                           rhs=QD[:, c * 64:(c + 1) * 64], start=False, stop=(hc == 1)),
                                 reads=[BSALL, BQD[tb]], writes=[BPS[6]])
                    S.op("act", lambda e: e.activation(out=OSQ, in_=PSF[6][:, :], func=AF.Square), reads=[BPS[6]], writes=[BOSQ])
                    S.op("pe", lambda e: e.matmul(PSF[7][:, :], lhsT=ONES, rhs=OSQ, start=True, stop=True), reads=[BOSQ, BC2], writes=[BPS[7]])
                    S.op("act", lambda e: e.activation(out=RS, in_=PSF[7][:, :], func=AF.Ln, bias=EPS_T[:, 1:2], scale=1.0 / 128.0), reads=[BPS[7], BCONST], writes=[BRS])
                    S.op("act", lambda e: e.activation(out=RS, in_=RS, func=AF.Exp, scale=-0.5), reads=[BRS], writes=[BRS])
                    S.op("dve", lambda e, cols=cols: e.scalar_tensor_tensor(out=ON[:, h, cols], in0=PSF[6][:, :], scalar=LBS[:, 32 + h:33 + h], in1=RS, op0=ALU.mult, op1=ALU.mult),
                         reads=[BPS[6], BRS, BC2], writes=[BON])

            NB = 32
            load_w(0)
            load_w(1)
            for tau in range(NB + 6):
                if 0 <= tau - 3 < NB:
                    st_Hk(tau - 3)
                    if (tau - 3) % 4 == 3:
                        head_tail((tau - 3) // 4)
                jj = tau - 6
                if 0 <= jj < NB:
                    scan_piece(jj // 4, jj % 4)
                    if not so:
                        out_block(jj // 4, jj % 4)
                if 0 <= tau - 2 < NB:
                    st_F(tau - 2)
                if 0 <= tau - 1 < NB:
                    st_D(tau - 1)
                if 0 <= tau - 2 < NB:
                    if not so:
                        st_Aq(tau - 2)
                    st_G(tau - 2)
                    if (tau - 2) % 4 == 3 and (tau - 2) // 4 + 2 < 8:
                        load_w((tau - 2) // 4 + 2)
                if 0 <= tau - 1 < NB:
                    st_Hv(tau - 1)
                if tau < NB:
                    st_A(tau)
            S.barrier()
            if so:
                S.dma("pool", "cc", lambda e: e.collective_compute("AllReduce", ALU.add, replica_groups=[[0, 1], [2, 3], [4, 5], [6, 7]],
                                                                 ins=[t_bin.ap().opt()], outs=[t_bout.ap().opt()]), reads=[BBIN], writes=[BBOUT], inc=1)
                al.pos = mark
                return
            al.pos = mark_on
            WOB = al.bf(8 * 1024).rearrange("p (a b) -> p a b", a=8)
            BWOB = [Buf() for _ in range(8)]
            for kc in range(8):
                S.dma("pool", f"c_hwo{kc % 4}", lambda e, kc=kc: e.dma_start(out=WOB[:, kc, :], in_=t_hwout.ap()[kc * 128:(kc + 1) * 128, :]), writes=[BWOB[kc]])
            alloc_ln(2)
            emit_outproj_residual(ON, BON, WOB, BWOB, 8)
            emit_ln_all()
            S.barrier()
            al.pos = mark

        if do_l0:
            emit_attention()
            if stage >= 2:
                emit_ffn(0, final=(mode == "l0"))
        BBIN = Buf()
        BBOUT = Buf()
        if do_l1:
            if mode == "l1":
                emit_load_x()
                emit_hgrn()
            else:
                emit_hgrn(final_scan_only=True, s0_mode="zero")
                emit_hgrn(s0_mode="bounce")
            emit_ffn(1, final=True)
        if mode == "l0" and stage < 2:
            for t in range(16):
                S.dma("sp", f"o_out{t % 4}", lambda e, t=t: e.dma_start(out=t_out.ap()[t * 128:(t + 1) * 128, :], in_=XTOK[:, t, :]), reads=[BXTOK[t]])
        S.wait_all_dma("sp", [f"o_out{i}" for i in range(4)] + (["sout"] if mode == "l1" else []))
        S.emit(st)
    return nc


def _tables(half):
    pos = np.arange(4096, dtype=np.float32) + np.float32(half * 2048 - 2048)
    inv = (np.float32(10000.0) ** (-np.arange(32, dtype=np.float32) * np.float32(2.0 / 64))).astype(np.float32)
    ang = pos[None, :].astype(np.float32) * inv[:, None]
    c = np.cos(ang).astype(np.float32)
    s = np.sin(ang).astype(np.float32)
    cs = np.concatenate([c, c, c, c], axis=0)
    sn = np.concatenate([-s, s, -s, s], axis=0)
    return np.ascontiguousarray(cs), np.ascontiguousarray(sn)


def _patterns(half):
    k = np.arange(128)[:, None]
    q = np.arange(128)[None, :]
    mprev = (k >= q).astype(np.float32)
    mcur = (k <= q).astype(np.float32)
    mph = mprev * np.float32(1.0 if half == 1 else 0.0)
    return np.ascontiguousarray(np.concatenate([mph, mcur, mprev, mcur], axis=1))


_CACHE = {}


def _get_nc(stage, mode):
    key = (stage, mode)
    if key not in _CACHE:
        _CACHE[key] = build(stage, mode)
    return _CACHE[key]


def _common_maps(ln_mix_g, ln_mix_b, ln_ffn_g, ln_ffn_b, ffn_w_up, ffn_w_down):
    lng = np.ascontiguousarray(np.stack([ln_mix_g[0], ln_ffn_g[0], ln_mix_g[1], ln_ffn_g[1]], axis=0), dtype=np.float32)
    lnb = np.ascontiguousarray(np.stack([ln_mix_b[0], ln_ffn_b[0], ln_mix_b[1], ln_ffn_b[1]], axis=0), dtype=np.float32)
    return {"ident": np.eye(128, dtype=np.float32), "lng": lng, "lnb": lnb,
            "wup": np.ascontiguousarray(ffn_w_up, dtype=np.float32), "wdn": np.ascontiguousarray(ffn_w_down, dtype=np.float32)}


def _l0_maps(x, attn_w_in, attn_w_out):
    perm = np.zeros((128, 128), np.float32)
    for m in range(128):
        perm[m ^ 32, m] = 1.0
    awin = np.ascontiguousarray(attn_w_in[0], dtype=np.float32)
    awout = np.ascontiguousarray(attn_w_out[0], dtype=np.float32)
    maps = []
    for c in range(8):
        b, half = divmod(c, 2)
        xh = np.ascontiguousarray(x[b, 0:NT]) if half == 1 else np.zeros((NT, D), np.float32)
        cs, sn = _tables(half)
        maps.append({"xh": xh, "cs": cs, "sn": sn, "pats": _patterns(half), "perm": perm, "awin": awin, "awout": awout})
    return maps


def _l1_maps(hgrn_w_in, hgrn_w_out, hgrn_norm_g, lb_logits):
    hng = np.ascontiguousarray(hgrn_norm_g[0].reshape(8, 128).T, dtype=np.float32)
    lbl = np.ascontiguousarray(lb_logits.reshape(2, 8, 128).transpose(2, 0, 1).reshape(128, 16), dtype=np.float32)
    j = np.arange(128)[:, None]
    i = np.arange(128)[None, :]
    tri = ((j // 64 == i // 64) & (j <= i)).astype(np.float32)
    rst = np.ones((128, 512), np.float32)
    rst[:, ::64] = 0.0
    return {"hwin": np.ascontiguousarray(hgrn_w_in[0], dtype=np.float32), "hwout": np.ascontiguousarray(hgrn_w_out[0], dtype=np.float32),
            "hng": hng, "lbl": lbl, "tri": tri, "rst": rst}


def _run(nc, in_maps):
    import os
    ncores = int(os.environ.get("KDBG_NCORES", "8"))
    res = run_bass_kernel_spmd(nc, in_maps[:ncores], core_ids=list(range(ncores)))
    return res.results


def kernel(x, attn_w_in, attn_w_out, hgrn_w_in, hgrn_w_out, hgrn_norm_g, lb_logits,
           ln_mix_g, ln_mix_b, ln_ffn_g, ln_ffn_b, ffn_w_up, ffn_w_down, _stage=99, _mode="full", _x1=None):
    x = np.ascontiguousarray(x, dtype=np.float32)
    B = x.shape[0]
    common = _common_maps(ln_mix_g, ln_mix_b, ln_ffn_g, ln_ffn_b, ffn_w_up, ffn_w_down)
    if _mode == "full":
        l0 = _l0_maps(x, attn_w_in, attn_w_out)
        l1 = _l1_maps(hgrn_w_in, hgrn_w_out, hgrn_norm_g, lb_logits)
        in_maps = []
        for c in range(8):
            b, half = divmod(c, 2)
            m = dict(common)
            m.update(l0[c])
            m.update(l1)
            m["xo"] = np.ascontiguousarray(x[b, half * NT:(half + 1) * NT])
            fl = np.zeros((128, 2), np.float32)
            fl[:, 0] = 1.0 if half == 0 else 0.0
            fl[:, 1] = 1.0 if half == 1 else 0.0
            m["flags"] = fl
            in_maps.append(m)
        r = _run(_get_nc(99, "full"), in_maps)
        out = np.zeros((B, 2 * NT, D), np.float32)
        for c in range(8):
            b, half = divmod(c, 2)
            out[b, half * NT:(half + 1) * NT] = r[c]["out"]
        return out
    if _mode in ("l0", "chain"):
        l0 = _l0_maps(x, attn_w_in, attn_w_out)
        in_maps = []
        for c in range(8):
            b, half = divmod(c, 2)
            m = dict(common)
            m.update(l0[c])
            m["xo"] = np.ascontiguousarray(x[b, half * NT:(half + 1) * NT])
            in_maps.append(m)
        r0 = _run(_get_nc(_stage, "l0"), in_maps)
        x1 = [r0[c]["out"] for c in range(len(r0))]
        if _mode == "l0":
            return np.concatenate(x1, axis=0).reshape(-1, 2 * NT, D) if len(x1) == 8 else np.concatenate(x1, axis=0)
    else:
        x1 = [np.ascontiguousarray(_x1[c // 2, (c % 2) * NT:(c % 2 + 1) * NT]) for c in range(8)]
    l1 = _l1_maps(hgrn_w_in, hgrn_w_out, hgrn_norm_g, lb_logits)
    nc1 = _get_nc(99, "l1")
    zeros = np.zeros((8, 128, 128), np.float32)
    in_maps = []
    for c in range(len(x1)):
        m = dict(common)
        m.update(l1)
        m["xo"] = x1[c]
        m["s0"] = zeros
        in_maps.append(m)
    ra = _run(nc1, in_maps)
    for c in range(len(x1)):
        if c % 2 == 1:
            in_maps[c]["s0"] = np.ascontiguousarray(ra[c - 1]["sout"])
    rb = _run(nc1, in_maps)
    outs = [rb[c]["out"] for c in range(len(rb))]
    if len(outs) < 8:
        return np.concatenate(outs, axis=0)
    out = np.zeros((B, 2 * NT, D), np.float32)
    for c in range(8):
        b, half = divmod(c, 2)
        out[b, half * NT:(half + 1) * NT] = outs[c]
    return out
```

```python
import contextlib
import numpy as np
import concourse.bass as bass
import concourse.mybir as mybir
from concourse.bass_utils import run_bass_kernel_spmd

F32 = mybir.dt.float32
BF16 = mybir.dt.bfloat16
AF = mybir.ActivationFunctionType
ALU = mybir.AluOpType

NT = 2048
D = 1024
ALPHA = 4.0 ** 0.25
LN_EPS = 1e-5
RMS_EPS = 1e-6
DIL = (1, 4, 16)
ENGS = ("pe", "act", "dve", "pool", "sp")
EPOCH = 20000


class Buf:
    __slots__ = ("lw", "rd", "excl")

    def __init__(self, excl=False):
        self.lw = None
        self.rd = []
        self.excl = excl


class Op:
    __slots__ = ("id", "eng", "fn", "deps", "sem", "inc", "semval", "cost", "seg", "pos", "needed", "cnt", "start", "fin", "cp", "tset")


DEF_COST = {"pe": 0.28, "act": 0.62, "dve": 0.68, "pool": 1.3, "sp": 0.1}


class _Probe:
    def __getattr__(self, name):
        def f(*a, **k):
            return (name, a, k)
        return f


def _free_elems(ap):
    try:
        n = 1
        for st_, cnt_ in list(ap.ap)[1:]:
            n *= int(cnt_)
        return n, int(list(ap.ap)[0][1])
    except Exception:
        return 512, 128


_TSET = {"Exp": "E", "Ln": "E", "Sigmoid": "S", "Silu": "L", "Sqrt": "Q"}


def _act_table(fn):
    try:
        name, a, k = fn(_Probe())
        f = k.get("func")
        if f is None:
            return None
        nm = getattr(f, "name", None) or str(f).split(".")[-1]
        return _TSET.get(nm)
    except Exception:
        return None


def _est_cost(eng, fn, is_dma):
    try:
        name, a, k = fn(_Probe())
    except Exception:
        return 3.0 if is_dma else DEF_COST[eng]
    out = k.get("out", a[0] if a else None)
    n, parts = _free_elems(out) if out is not None else (512, 128)
    if is_dma:
        if name == "collective_compute":
            return 40.0
        return 2.0 + n * parts * 4 / 300e3
    if eng == "pe":
        return 0.10 if name == "transpose" else 0.065 + n * 0.00043
    if eng == "act":
        return 0.24 + n * 0.00072
    if eng == "dve":
        c = 0.09 + n * 0.00115
        if name == "reciprocal":
            c = 0.1 + n * 0.0038
        elif name == "tensor_tensor_scan":
            c = 0.1 + n * 0.0022
        return c
    if eng == "pool":
        if name == "tensor_copy":
            return 0.15 + n * 0.0037
        return 0.15 + n * 0.0022
    return DEF_COST[eng]


class Sched:
    def __init__(self, nc):
        self.nc = nc
        self.ops = []
        self.seg = 0
        self.dma_issued = {}
        self.dma_inc = {}
        self.dma_last = {}
        self.final_waits = []
        self.cur_delta = 0.0
        self.delta_by_seg = {}

    def _mk(self, eng, fn, reads, writes, cost, is_dma=False):
        op = Op()
        op.id = len(self.ops)
        op.eng = eng
        op.fn = fn
        op.sem = None
        op.inc = 0
        op.semval = 0
        op.seg = self.seg
        self.delta_by_seg[self.seg] = self.cur_delta
        op.needed = False
        op.cnt = None
        op.cost = _est_cost(eng, fn, is_dma) if cost is None else cost
        op.tset = _act_table(fn) if (eng == "act" and not is_dma) else None
        deps = set()
        for b in reads:
            if b.lw is not None:
                deps.add(b.lw)
            if b.excl:
                for r in b.rd:
                    if self.ops[r].eng != eng or self.ops[r].sem is not None:
                        deps.add(r)
        for b in writes:
            if b.lw is not None:
                deps.add(b.lw)
            deps.update(b.rd)
        op.deps = deps
        self.ops.append(op)
        for b in reads:
            b.rd.append(op.id)
        for b in writes:
            b.lw = op.id
            b.rd = []
        return op

    def op(self, eng, fn, reads=(), writes=(), cost=None):
        return self._mk(eng, fn, reads, writes, cost)

    def dma(self, qeng, sem, fn, reads=(), writes=(), inc=16, cost=None):
        op = self._mk(qeng, fn, reads, writes, cost, is_dma=True)
        n = self.dma_issued.get(sem, 0) + 1
        self.dma_issued[sem] = n
        self.dma_inc[sem] = inc
        op.sem = sem
        op.inc = inc
        op.semval = n * inc
        if sem in self.dma_last:
            op.deps.add(self.dma_last[sem])
        self.dma_last[sem] = op.id
        return op

    def barrier(self):
        self.seg += 1

    def wait_all_dma(self, qeng, sems):
        for s in sems:
            self.final_waits.append((qeng, s, self.dma_issued[s] * self.dma_inc[s]))

    def _schedule(self, seg_ops):
        import heapq
        ops = self.ops
        ids = [o.id for o in seg_ops]
        inseg = set(ids)
        succ = {i: [] for i in ids}
        npred = {}
        for o in seg_ops:
            d = [x for x in o.deps if x in inseg]
            npred[o.id] = len(d)
            for x in d:
                succ[x].append(o.id)
        for i in reversed(ids):
            o = ops[i]
            o.cp = o.cost + max([ops[j].cp for j in succ[i]], default=0.0)
        free = {e: 0.0 for e in ENGS}
        ready_t = {i: 0.0 for i in ids}
        order = {e: [] for e in ENGS}
        ready = {e: [] for e in ENGS}
        for i in ids:
            if npred[i] == 0:
                ready[ops[i].eng].append(i)
        remaining = len(ids)
        cur_tab = [None]
        import os
        TAB = float(os.environ.get("KS_TAB", "1.3"))
        HOP = float(os.environ.get("KS_HOP", "0.25"))
        DELTA = self.delta_by_seg.get(seg_ops[0].seg, 0.0) if seg_ops else 0.0
        if "KS_DELTA" in os.environ:
            DELTA = float(os.environ["KS_DELTA"])

        def pen(i):
            o_ = ops[i]
            return TAB if (o_.tset is not None and o_.tset != cur_tab[0]) else 0.0

        while remaining:
            best = None
            for e in ENGS:
                if not ready[e]:
                    continue
                if e == "act":
                    stf = lambda i: max(free[e], ready_t[i]) + pen(i)
                else:
                    stf = lambda i: max(free[e], ready_t[i])
                if DELTA > 0:
                    m0 = min(stf(i) for i in ready[e])
                    c = min((i for i in ready[e] if stf(i) <= m0 + DELTA), key=lambda i: (-ops[i].cp, i))
                else:
                    c = min(ready[e], key=lambda i: (stf(i), -ops[i].cp, i))
                st_ = max(free[e], ready_t[c])
                if best is None or (st_, -ops[c].cp, c) < best[0]:
                    best = ((st_, -ops[c].cp, c), e, c, st_)
            _, e, c, st_ = best
            ready[e].remove(c)
            o = ops[c]
            extra = 0.0
            if e == "act" and o.tset is not None:
                extra = pen(c)
                cur_tab[0] = o.tset
            st_ += extra
            o.start = st_
            issue = 0.08 if o.sem is not None else o.cost
            free[e] = st_ + issue
            o.fin = st_ + o.cost + HOP
            order[e].append(c)
            remaining -= 1
            for j in succ[c]:
                ready_t[j] = max(ready_t[j], o.fin)
                npred[j] -= 1
                if npred[j] == 0:
                    ready[ops[j].eng].append(j)
        return order

    def emit(self, stack):
        nc = self.nc
        ops = self.ops
        nseg = self.seg + 1
        segs = [[] for _ in range(nseg)]
        for o in ops:
            segs[o.seg].append(o)
        final = {e: [] for e in ENGS}
        for si in range(nseg):
            order = self._schedule(segs[si])
            for e in ENGS:
                for k, i in enumerate(order[e]):
                    final[e].append((ops[i], si if (k == 0 and si > 0) else None))
        for e in ENGS:
            for p, (o, _) in enumerate(final[e]):
                o.pos = p
        last_in_seg = [{} for _ in range(nseg)]
        dma_in_seg = [{} for _ in range(nseg)]
        for e in ENGS:
            for (o, _) in final[e]:
                if o.sem is None:
                    last_in_seg[o.seg][e] = o
                else:
                    dma_in_seg[o.seg][o.sem] = max(dma_in_seg[o.seg].get(o.sem, 0), o.semval)
        waits = {}
        waited = {e: {} for e in ENGS}
        for e in ENGS:
            w = waited[e]
            for (o, barrier_seg) in final[e]:
                need = {}

                def add(key, val):
                    if need.get(key, -1) < val:
                        need[key] = val

                if barrier_seg is not None:
                    for ps in range(barrier_seg):
                        for e2, lo in last_in_seg[ps].items():
                            if e2 != e or e != "pe":
                                add(("eng", e2), lo.pos)
                        for sname, v in dma_in_seg[ps].items():
                            add(("dma", sname), v)
                for d in o.deps:
                    od = ops[d]
                    if od.seg != o.seg:
                        continue
                    if od.sem is not None:
                        add(("dma", od.sem), od.semval)
                    elif od.eng == e and e == "pe" and o.sem is None:
                        continue
                    else:
                        add(("eng", od.eng), od.pos)
                lst = []
                for key, val in need.items():
                    if w.get(key, -1) >= val:
                        continue
                    w[key] = val
                    lst.append((key, val))
                    if key[0] == "eng":
                        final[key[1]][val][0].needed = True
                waits[o.id] = lst
        eng_sems = {}
        for e in ENGS:
            c = 0
            for (o, _) in final[e]:
                if o.needed and o.sem is None:
                    c += 1
                    o.cnt = c
            nep = c // EPOCH + 1
            eng_sems[e] = [stack.enter_context(nc.semaphore(f"s_{e}_{i}")) for i in range(nep)]
        dma_sems = {s: stack.enter_context(nc.semaphore(f"d_{s}")) for s in self.dma_issued}

        def resolve(key, val):
            if key[0] == "dma":
                return dma_sems[key[1]], val
            c = final[key[1]][val][0].cnt
            ep, r = divmod(c - 1, EPOCH)
            return eng_sems[key[1]][ep], r + 1

        def run(e, engobj):
            for (o, _) in final[e]:
                for (key, val) in waits[o.id]:
                    s, v = resolve(key, val)
                    engobj.wait_ge(s, v)
                r = o.fn(engobj)
                if o.sem is not None:
                    r.then_inc(dma_sems[o.sem], o.inc)
                elif o.needed:
                    ep = (o.cnt - 1) // EPOCH
                    r.then_inc(eng_sems[e][ep], 1)
            for (qe, s, v) in self.final_waits:
                if qe == e:
                    engobj.wait_ge(dma_sems[s], v)

        with nc.Block() as block:
            @block.tensor
            def _(t):
                run("pe", t)

            @block.scalar
            def _(t):
                run("act", t)

            @block.vector
            def _(t):
                run("dve", t)

            @block.gpsimd
            def _(t):
                run("pool", t)

            @block.sync
            def _(t):
                run("sp", t)


def sl(start, n, step=1):
    return slice(start, start + (n - 1) * step + 1, step)


class Alloc:
    def __init__(self, A, Ab, nbytes):
        self.A = A
        self.Ab = Ab
        self.n = nbytes
        self.pos = 0

    def f32(self, n):
        self.pos = (self.pos + 63) // 64 * 64
        o = self.pos
        self.pos += n * 4
        assert self.pos <= self.n, ("arena overflow", self.pos, self.n)
        return self.A[:, o // 4:o // 4 + n]

    def bf(self, n):
        self.pos = (self.pos + 63) // 64 * 64
        o = self.pos
        n2 = (n + 1) // 2 * 2
        self.pos += n2 * 2
        assert self.pos <= self.n, ("arena overflow", self.pos, self.n)
        return self.Ab[:, o // 2:o // 2 + n]


def build(stage=99, mode="full"):
    nc = bass.Bass("TRN2", target_bir_lowering=False)

    def din(name, shape):
        return nc.dram_tensor(name, shape, F32, kind="ExternalInput")

    def dout(name, shape):
        return nc.dram_tensor(name, shape, F32, kind="ExternalOutput")

    do_l0 = mode in ("full", "l0")
    do_l1 = mode in ("full", "l1")
    t_xo = din("xo", [NT, D])
    t_ident = din("ident", [128, 128])
    t_lng = din("lng", [4, D])
    t_lnb = din("lnb", [4, D])
    t_wup = din("wup", [2, D, 4 * D])
    t_wdn = din("wdn", [2, 4 * D, D])
    if do_l0:
        t_xh = din("xh", [NT, D])
        t_cs = din("cs", [128, 4096])
        t_sn = din("sn", [128, 4096])
        t_pats = din("pats", [128, 512])
        t_perm = din("perm", [128, 128])
        t_win = din("awin", [D, 9 * D])
        t_wout = din("awout", [D, D])
    if do_l1:
        t_hwin = din("hwin", [D, 3 * D])
        t_hwout = din("hwout", [D, D])
        t_hng = din("hng", [128, 8])
        t_lbl = din("lbl", [128, 16])
        if mode == "l1":
            t_s0 = din("s0", [8, 128, 128])
            t_sout = dout("sout", [8, 128, 128])
        else:
            t_flags = din("flags", [128, 2])
            t_bin = nc.dram_tensor("sx_in", [1024, 128], F32)
            t_bout = nc.dram_tensor("sx_out", [1024, 128], F32)
        t_tri = din("tri", [128, 128])
        t_rst = din("rst", [128, 512])
    t_out = dout("out", [NT, D])

    st = contextlib.ExitStack()
    with st:
        NF = 53200
        A = st.enter_context(nc.sbuf_tensor("arena", [128, NF], F32))
        Ab = A.bitcast(BF16)
        al = Alloc(A, Ab, NF * 4)
        PSF = [st.enter_context(nc.psum_tensor(f"ps{i}", [128, 512], F32)) for i in range(8)]
        PSB = [p.bitcast(BF16) for p in PSF]
        BPS = [Buf(excl=True) for _ in range(8)]
        S = Sched(nc)
        out_sems = []

        XT = al.bf(8 * NT).rearrange("p (a b) -> p a b", a=8)
        BXT = [Buf() for _ in range(4)]
        XTOK = al.f32(16 * D).rearrange("p (a b) -> p a b", a=16)
        BXTOK = [Buf() for _ in range(16)]
        xtok_off = 8 * NT * 2
        IDENT = al.bf(128)
        BID = Buf()
        S.dma("pool", "c_id", lambda e: e.dma_start(out=IDENT, in_=t_ident.ap()), writes=[BID])
        NLN = 4
        STATS_L = [al.f32(32) for _ in range(NLN)]
        BSTATS_L = [Buf() for _ in range(NLN)]
        LNR = {}
        BGB = Buf()
        BGB2 = Buf()
        BLNT_L = [Buf() for _ in range(NLN)]
        BLNB_L = [Buf() for _ in range(NLN)]
        lncnt = {"n": 0}

        def alloc_ln(row):
            GB = al.f32(2 * D).rearrange("p (a b) -> p a b", a=2)
            LNR["GB"] = GB
            LNR["LNT"] = [al.f32(D) for _ in range(NLN)]
            LNR["LNB"] = [al.bf(D) for _ in range(NLN)]

            def bc(t):
                return bass.AP(t, row * D, [[0, 128], [1, D]])
            S.dma("sp", "c_g", lambda e: e.dma_start(out=GB[:, 0, :], in_=bc(t_lng)), writes=[BGB])
            S.dma("sp", "c_b", lambda e: e.dma_start(out=GB[:, 1, :], in_=bc(t_lnb)), writes=[BGB2])

        def emit_ln_all(final=False, wide=False):
            LNPB = (6, 7, 0, 1) if wide else (6, 7)
            GB = LNR["GB"]

            def bufs(t):
                k_ = t % NLN
                return LNR["LNT"][k_], LNR["LNB"][k_], STATS_L[k_], BSTATS_L[k_], BLNT_L[k_], BLNB_L[k_]

            def s1(t):
                z = XTOK[:, t, :]
                bz = BXTOK[t]
                LNT, LNB, STATS, BSTATS, BLNT, BLNB = bufs(t)
                S.op("dve", lambda e: e.bn_stats(out=STATS[:, 0:6], in_=z[:, 0:512]), reads=[bz], writes=[BSTATS])
                S.op("dve", lambda e: e.bn_stats(out=STATS[:, 6:12], in_=z[:, 512:1024]), reads=[bz], writes=[BSTATS])
                S.op("dve", lambda e: e.bn_aggr(out=STATS[:, 12:14], in_=STATS[:, 0:12]), reads=[BSTATS], writes=[BSTATS])
                S.op("act", lambda e: e.activation(out=STATS[:, 14:15], in_=STATS[:, 13:14], func=AF.Sqrt, bias=EPS_T[:, 0:1], scale=1.0),
                     reads=[BSTATS, BCONST], writes=[BSTATS])
                S.op("dve", lambda e: e.reciprocal(out=STATS[:, 15:16], in_=STATS[:, 14:15]), reads=[BSTATS], writes=[BSTATS])
                S.op("dve", lambda e: e.tensor_scalar(out=STATS[:, 16:17], in0=STATS[:, 12:13], scalar1=STATS[:, 15:16], scalar2=-1.0,
                                                      op0=ALU.mult, op1=ALU.mult), reads=[BSTATS], writes=[BSTATS])
                S.op("act", lambda e: e.activation(out=LNT, in_=z, func=AF.Identity, bias=STATS[:, 16:17], scale=STATS[:, 15:16]),
                     reads=[bz, BSTATS], writes=[BLNT])

            def s2(t):
                z = XTOK[:, t, :]
                bz = BXTOK[t]
                LNT, LNB, STATS, BSTATS, BLNT, BLNB = bufs(t)
                S.op("dve", lambda e: e.tensor_tensor(out=LNT, in0=LNT, in1=GB[:, 0, :], op=ALU.mult), reads=[BLNT, BGB], writes=[BLNT])
                S.op("pool", lambda e: e.tensor_tensor(out=z, in0=LNT, in1=GB[:, 1, :], op=ALU.add), reads=[BLNT, BGB2], writes=[bz])

            def s3(t):
                z = XTOK[:, t, :]
                bz = BXTOK[t]
                LNT, LNB, STATS, BSTATS, BLNT, BLNB = bufs(t)
                if final:
                    S.dma("sp", f"o_out{t % 4}", lambda e: e.dma_start(out=t_out.ap()[t * 128:(t + 1) * 128, :], in_=z), reads=[bz])
                    return
                S.op("act", lambda e: e.copy(out=LNB, in_=z), reads=[bz], writes=[BLNB])
                pb = LNPB[t % len(LNPB)]
                for kc in range(8):
                    S.op("pe", lambda e, kc=kc: e.transpose(out=PSB[pb][:, kc * 128:(kc + 1) * 128], in_=LNB[:, kc * 128:(kc + 1) * 128], identity=IDENT),
                         reads=[BLNB, BID], writes=[BPS[pb]])

            def s4(t):
                pb = LNPB[t % len(LNPB)]
                dst = XT[:, :, t * 128:(t + 1) * 128]
                src = PSB[pb][:, 0:1024].rearrange("p (a b) -> p a b", a=8)
                S.op("dve", lambda e: e.tensor_copy(out=dst, in_=src), reads=[BPS[pb]], writes=[BXT[t // 4]])

            for tau in range(16 + 3):
                if tau < 16:
                    s1(tau)
                if 0 <= tau - 1 < 16:
                    s2(tau - 1)
                if 0 <= tau - 2 < 16:
                    s3(tau - 2)
                if 0 <= tau - 3 < 16 and not final:
                    s4(tau - 3)

        EPS_T = al.f32(2)
        BCONST = Buf()
        S.op("dve", lambda e: e.memset(EPS_T[:, 0:1], LN_EPS), writes=[BCONST])
        S.op("dve", lambda e: e.memset(EPS_T[:, 1:2], RMS_EPS), writes=[BCONST])
        base_pos = al.pos

        def emit_outproj_residual(srcT, bsrc, WB, bwb, nkc):
            for t in range(16):
                for hf in range(2):
                    pb = (4, 5, 2, 3)[(2 * t + hf) % 4]
                    for kc in range(nkc):
                        S.op("pe", lambda e, kc=kc, hf=hf, pb=pb, t=t: e.matmul(PSF[pb][:, :], lhsT=srcT[:, kc, t * 128:(t + 1) * 128],
                                                                               rhs=WB[:, kc, hf * 512:(hf + 1) * 512], start=(kc == 0), stop=(kc == nkc - 1)),
                             reads=[bsrc, bwb[kc]], writes=[BPS[pb]])
                    zz = XTOK[:, t, hf * 512:(hf + 1) * 512]
                    S.op("dve", lambda e, zz=zz, pb=pb: e.scalar_tensor_tensor(out=zz, in0=zz, scalar=ALPHA, in1=PSF[pb][:, :], op0=ALU.mult, op1=ALU.add),
                         reads=[BXTOK[t], BPS[pb]], writes=[BXTOK[t]])

        def emit_ffn(layer, final):
            mark = al.pos
            HT = [al.bf(4 * NT).rearrange("p (a b) -> p a b", a=4) for _ in range(2)]
            BHT = [[Buf() for _ in range(4)] for _ in range(2)]
            WU = [al.bf(8 * 512).rearrange("p (a b) -> p a b", a=8) for _ in range(2)]
            WD = [al.bf(4 * 1024).rearrange("p (a b) -> p a b", a=4) for _ in range(2)]
            BWU = [Buf(), Buf()]
            BWD = [Buf(), Buf()]
            RL = [al.f32(512) for _ in range(2)]
            BRL = [Buf(), Buf()]
            alloc_ln(1 + 2 * layer)

            def load_w(fg):
                s = fg % 2
                su = t_wup.ap()[layer, :, fg * 512:(fg + 1) * 512].rearrange("(kc p) c -> p kc c", p=128)
                sd = t_wdn.ap()[layer, fg * 512:(fg + 1) * 512, :].rearrange("(fc p) c -> p fc c", p=128)
                S.dma("pool", f"wu{s}", lambda e: e.dma_start(out=WU[s], in_=su), writes=[BWU[s]])
                S.dma("pool", f"wd{s}", lambda e: e.dma_start(out=WD[s], in_=sd), writes=[BWD[s]])

            def up(fg):
                s = fg % 2
                n = 0
                for tb in range(4):
                    for fc in range(4):
                        pb = (0, 1, 4)[n % 3]
                        ri = n % 2
                        n += 1
                        for kc in range(8):
                            S.op("pe", lambda e, kc=kc, fc=fc, tb=tb, pb=pb: e.matmul(PSF[pb][:, :], lhsT=WU[s][:, kc, fc * 128:(fc + 1) * 128],
                                                                                   rhs=XT[:, kc, tb * 512:(tb + 1) * 512], start=(kc == 0), stop=(kc == 7)),
                                 reads=[BWU[s], BXT[tb]], writes=[BPS[pb]])
                        S.op("act", lambda e, pb=pb, ri=ri: e.activation(out=RL[ri], in_=PSF[pb][:, :], func=AF.Relu), reads=[BPS[pb]], writes=[BRL[ri]])
                        S.op("pool", lambda e, ri=ri, fc=fc, tb=tb: e.tensor_tensor(out=HT[s][:, fc, tb * 512:(tb + 1) * 512], in0=RL[ri], in1=RL[ri], op=ALU.mult),
                             reads=[BRL[ri]], writes=[BHT[s][tb]])

            def down(fg):
                s = fg % 2
                n = 0
                for t in range(16):
                    for hf in range(2):
                        pb = (2, 3, 5)[n % 3]
                        n += 1
                        for fc in range(4):
                            S.op("pe", lambda e, fc=fc, hf=hf, t=t, pb=pb: e.matmul(PSF[pb][:, :], lhsT=HT[s][:, fc, t * 128:(t + 1) * 128],
                                                                                 rhs=WD[s][:, fc, hf * 512:(hf + 1) * 512], start=(fc == 0), stop=(fc == 3)),
                                 reads=[BHT[s][t // 4], BWD[s]], writes=[BPS[pb]])
                        zz = XTOK[:, t, hf * 512:(hf + 1) * 512]
                        if fg == 0:
                            S.op("dve", lambda e, zz=zz, pb=pb: e.scalar_tensor_tensor(out=zz, in0=zz, scalar=ALPHA, in1=PSF[pb][:, :], op0=ALU.mult, op1=ALU.add),
                                 reads=[BXTOK[t], BPS[pb]], writes=[BXTOK[t]])
                        else:
                            S.op("dve", lambda e, zz=zz, pb=pb: e.tensor_tensor(out=zz, in0=zz, in1=PSF[pb][:, :], op=ALU.add),
                                 reads=[BXTOK[t], BPS[pb]], writes=[BXTOK[t]])

            load_w(0)
            load_w(1)
            up(0)
            for fg in range(8):
                if fg + 1 < 8:
                    up(fg + 1)
                down(fg)
                if fg + 2 < 8:
                    load_w(fg + 2)
            emit_ln_all(final=final)
            S.barrier()
            al.pos = mark

        def emit_attention():
            mark = al.pos
            S.cur_delta = 0.0
            XH = Ab[:, xtok_off // 2: xtok_off // 2 + 8 * NT].rearrange("p (a b) -> p a b", a=8)
            BXH = [Buf() for _ in range(4)]
            o = xtok_off + 8 * NT * 2
            CS = A[:, o // 4: o // 4 + 4096]
            o += 4096 * 4
            SN = A[:, o // 4: o // 4 + 4096]
            BTAB = Buf()
            S.dma("sp", "c_cs", lambda e: e.dma_start(out=CS, in_=t_cs.ap()), writes=[BTAB])
            S.dma("sp", "c_sn", lambda e: e.dma_start(out=SN, in_=t_sn.ap()), writes=[BTAB])
            OT = al.bf(8 * NT).rearrange("p (a b) -> p a b", a=8)
            BOT = Buf()
            mark_ot = al.pos
            OACC = [al.f32(NT) for _ in range(2)]
            BOACC = [[Buf(), Buf()] for _ in range(2)]
            QT = al.bf(NT)
            BQT = [Buf() for _ in range(4)]
            KT = al.bf(4096)
            BKT = [Buf() for _ in range(8)]
            VT = al.bf(4096)
            BVT = [Buf() for _ in range(8)]
            VAUG = al.bf(32 * 256).rearrange("p (t h c) -> p t h c", t=32, h=2)
            BVA = Buf()
            WQ = [[al.bf(8 * 128).rearrange("p (a b) -> p a b", a=8) for _ in range(3)] for _ in range(2)]
            BW = [[Buf() for _ in range(3)] for _ in range(2)]
            QB = [al.bf(512) for _ in range(2)]
            BQB = [Buf(), Buf()]
            RA = [al.f32(512) for _ in range(2)]
            BRA = [Buf(), Buf()]
            RB = [al.f32(512) for _ in range(2)]
            BRB = [Buf(), Buf()]
            NPT = 3
            PT = [al.bf(512) for _ in range(NPT)]
            BPT = [Buf() for _ in range(NPT)]
            PATS = al.bf(512)
            BPAT = Buf()
            PERM = al.bf(128)
            BPERM = Buf()
            NXIN = 4
            XIN = [OT[:, i, 0:1024] for i in range(NXIN)]
            BXIN = [Buf() for _ in range(NXIN)]
            S.dma("pool", "c_pat", lambda e: e.dma_start(out=PATS, in_=t_pats.ap()), writes=[BPAT])

            def pat_views(pat, pt):
                if pat == 1:
                    return PATS[:, 0:512], pt
                if pat == 0:
                    m = PATS[:, 256:512]
                    return bass.AP(m.tensor, m.offset, [m.ap[0], [0, 2], [1, 256]]), pt.rearrange("p (a b) -> p a b", a=2)
                m = PATS[:, (pat - 2) * 64:(pat - 2) * 64 + 64]
                return (bass.AP(m.tensor, m.offset, [m.ap[0], [0, 4], [128, 2], [1, 64]]),
                        pt.rearrange("p (a b c) -> p a b c", a=4, b=2))
            S.dma("pool", "c_perm", lambda e: e.dma_start(out=PERM, in_=t_perm.ap()), writes=[BPERM])
            S.op("pool", lambda e: e.memset(VAUG[:, :, :, 64:128], 1.0), writes=[BVA])

            for ti, t in enumerate(list(range(16, 32)) + list(range(15, -1, -1))):
                srcd = (t_xh if t < 16 else t_xo).ap()[(t % 16) * 128:(t % 16 + 1) * 128, :]
                s = ti % NXIN
                S.dma("pool", f"xin{s}", lambda e, s=s, srcd=srcd: e.dma_start(out=XIN[s], in_=srcd), writes=[BXIN[s]])
                pb = 6 + (t % 2)
                for kc in range(8):
                    S.op("pe", lambda e, kc=kc, s=s, pb=pb: e.transpose(out=PSB[pb][:, kc * 128:(kc + 1) * 128], in_=XIN[s][:, kc * 128:(kc + 1) * 128], identity=IDENT),
                         reads=[BXIN[s], BID], writes=[BPS[pb]])
                tt = t % 16
                dst = (XH if t < 16 else XT)[:, :, tt * 128:(tt + 1) * 128]
                bd = (BXH if t < 16 else BXT)[tt // 4]
                src = PSB[pb][:, 0:1024].rearrange("p (a b) -> p a b", a=8)
                eng = "act" if t % 2 else "dve"
                if eng == "act":
                    S.op("act", lambda e, dst=dst, src=src: e.copy(out=dst, in_=src), reads=[BPS[pb]], writes=[bd])
                else:
                    S.op("dve", lambda e, dst=dst, src=src: e.tensor_copy(out=dst, in_=src), reads=[BPS[pb]], writes=[bd])

            def xcols(kc, c0, n):
                if c0 < 2048:
                    return XH[:, kc, c0:c0 + n], BXH[c0 // 512]
                return XT[:, kc, c0 - 2048:c0 - 2048 + n], BXT[(c0 - 2048) // 512]

            def load_w(it):
                hp, g = divmod(it, 3)
                s = it % 2
                for wh in range(3):
                    c0 = g * 3072 + wh * 1024 + hp * 128
                    src = t_win.ap()[:, c0:c0 + 128].rearrange("(kc p) c -> p kc c", p=128)
                    S.dma("pool", f"wq{s}{wh}", lambda e, s=s, wh=wh, src=src: e.dma_start(out=WQ[s][wh], in_=src), writes=[BW[s][wh]])

            cnt = {"pj": 0, "st": 0}
            import os
            ROT_ADD_ENG = "dve" if "addondve" in os.environ.get("KDBG", "") else "pool"

            def proj_blocks(g, wh):
                d = DIL[g]
                if wh == 0:
                    return [(2048 + 512 * i, 512) for i in range(4)]
                c = 2048 - 128 * d
                out = []
                if c % 512:
                    out.append((c, 128))
                    c += 128
                while c < 4096:
                    out.append((c, 512))
                    c += 512
                return out

            def cm_views(T, g, wh, c0, n):
                d = DIL[g]
                if wh == 0:
                    u0, W = c0 - 2048, 2048 // d
                else:
                    u0, W = c0 - (2048 - 128 * d), (16 // d + 1) * 128
                if d == 1:
                    return T[:, u0:u0 + n], (lambda a: a)
                base = T[:, u0 // d:u0 // d + 1]
                dst = bass.AP(base.tensor, base.offset, [base.ap[0], [W, d], [1, n // d]])
                return dst, (lambda a: a.rearrange("p (i r) -> p r i", r=d))

            def project(it, wh):
                hp, g = divmod(it, 3)
                s = it % 2
                for (c0, n) in proj_blocks(g, wh):
                    pb = cnt["pj"] % 2
                    cnt["pj"] += 1
                    for kc in range(8):
                        xa, xb_ = xcols(kc, c0, n)
                        S.op("pe", lambda e, kc=kc, xa=xa, pb=pb, n=n: e.matmul(PSF[pb][:, 0:n], lhsT=WQ[s][wh][:, kc, :], rhs=xa, start=(kc == 0), stop=(kc == 7)),
                             reads=[BW[s][wh], xb_], writes=[BPS[pb]])
                    if wh == 2:
                        vdst, vview = cm_views(VT, g, 2, c0, n)
                        S.op("act", lambda e, pb=pb, n=n, vdst=vdst, vview=vview: e.copy(out=vdst, in_=vview(PSF[pb][:, 0:n])), reads=[BPS[pb]], writes=[BVT[0]])
                        continue
                    S.op("act", lambda e, pb=pb, n=n: e.copy(out=QB[pb][:, 0:n], in_=PSF[pb][:, 0:n]), reads=[BPS[pb]], writes=[BQB[pb]])
                    DBGK = os.environ.get("KDBG", "")
                    if "rot_stop1" in DBGK:
                        continue
                    if "rot_noperm" not in DBGK:
                        S.op("pe", lambda e, pb=pb, n=n: e.matmul(PSF[2][:, 0:n], lhsT=PERM, rhs=QB[pb][:, 0:n], start=True, stop=True),
                             reads=[BQB[pb], BPERM], writes=[BPS[2]])
                    if "rot_nomul" in DBGK:
                        S.op("dve", lambda e, pb=pb, c0=c0, n=n: e.tensor_copy(out=RA[pb][:, 0:n], in_=PSF[pb][:, 0:n]),
                             reads=[BPS[pb], BQB[pb]] + ([b for bb in BW for b in bb] if "waitw" in DBGK else []), writes=[BRA[pb]])
                        S.op("dve", lambda e, pb=pb, c0=c0, n=n: e.tensor_copy(out=RB[pb][:, 0:n], in_=PSF[2 if "rot_noperm" not in DBGK else pb][:, 0:n]),
                             reads=[BPS[2]], writes=[BRB[pb]])
                    else:
                        S.op("dve", lambda e, pb=pb, c0=c0, n=n: e.tensor_tensor(out=RA[pb][:, 0:n], in0=PSF[pb][:, 0:n], in1=CS[:, c0:c0 + n], op=ALU.mult),
                             reads=[BPS[pb], BTAB], writes=[BRA[pb]])
                        S.op("dve", lambda e, pb=pb, c0=c0, n=n: e.tensor_tensor(out=RB[pb][:, 0:n], in0=PSF[2 if "rot_noperm" not in DBGK else pb][:, 0:n], in1=SN[:, c0:c0 + n], op=ALU.mult),
                             reads=[BPS[2], BTAB], writes=[BRB[pb]])
                    if "rot_stop2" in DBGK:
                        continue
                    if wh == 0:
                        dst, vw = cm_views(QT, g, 0, c0, n)
                        bd = BQT[0]
                    else:
                        dst, vw = cm_views(KT, g, 1, c0, n)
                        bd = BKT[0]
                    S.op(ROT_ADD_ENG, lambda e, pb=pb, dst=dst, n=n, vw=vw: e.tensor_tensor(out=dst, in0=vw(RA[pb][:, 0:n]), in1=vw(RB[pb][:, 0:n]), op=ALU.add),
                         reads=[BRA[pb], BRB[pb]], writes=[bd])

            def build_vaug(g):
                d = DIL[g]
                nb = 16 // d
                tiles = [(r, j) for r in range(d) for j in range(nb + 1)]
                for i0 in range(0, len(tiles), 4):
                    grp = tiles[i0:i0 + 4]
                    pb = 6 + (i0 // 4) % 2
                    for k, (r, j) in enumerate(grp):
                        vt_ = r * (nb + 1) + j
                        src = VT[:, vt_ * 128:(vt_ + 1) * 128]
                        S.op("pe", lambda e, k=k, src=src, pb=pb: e.transpose(out=PSB[pb][:, k * 128:(k + 1) * 128], in_=src, identity=IDENT),
                             reads=BVT + [BID], writes=[BPS[pb]])
                    n = len(grp)
                    vt0 = i0
                    dst = VAUG[:, vt0:vt0 + n, :, 0:64]
                    src = PSB[pb][:, 0:n * 128].rearrange("p (t h c) -> p t h c", t=n, h=2)
                    S.op("dve", lambda e, dst=dst, src=src: e.tensor_copy(out=dst, in_=src), reads=[BPS[pb]], writes=[BVA])

            def qchunks(g, half):
                d = DIL[g]
                out = []
                if g == 0:
                    for b in range(1 + 8 * half, 9 + 8 * half):
                        tc = (b - 1) * 128 - half * 1024
                        out.append((0, b, 0, 128, [(tc // 512, tc % 512, 1, 128, 0)]))
                elif g == 1:
                    for r in range(4):
                        for b in (1 + 2 * half, 2 + 2 * half):
                            cmc = 256 * r + ((b - 1) % 2) * 128
                            out.append((r, b, 0, 128, [(cmc // 512, cmc % 512, 1, 128, 0)]))
                else:
                    for r in range(16):
                        cmc = 64 * r
                        out.append((r, 1, 64 * half, 64, [(cmc // 512, cmc % 512, 1, 64, 0)]))
                return out

            def attention_units(it, h):
                hp, g = divmod(it, 3)
                d = DIL[g]
                nb = 16 // d
                hs = slice(h * 64, (h + 1) * 64)
                units = []
                for half in range(2):
                    qc = qchunks(g, half)
                    per = 2 if g < 2 else 4
                    nun = len(qc) // per
                    for ui, u0 in enumerate(range(0, len(qc), per)):
                        unit = qc[u0:u0 + per]
                        if g == 2:
                            pat = 2 + half
                        elif g == 1:
                            pat = 1 if half == 0 else 0
                        else:
                            pat = 1 if (half == 0 and u0 == 0) else 0
                        mm = []
                        col = 0
                        for (r, b, i0, nq, pieces) in unit:
                            qs0 = r * (2048 // d) + (b - 1) * 128 + i0
                            qap = QT[hs, qs0:qs0 + nq]
                            for jj in (b - 1, b):
                                vt_ = r * (nb + 1) + jj
                                kap = KT[hs, vt_ * 128:(vt_ + 1) * 128]
                                mm.append((kap, qap, col, nq, r * (nb + 1) + jj, jj == b - 1, pieces))
                                col += nq
                        units.append(dict(h=h, half=half, g=g, pat=pat, mm=mm, ncol=col, last=(ui == nun - 1)))
                return units

            def emit_st(u, i):
                sb_ = (3, 7)[i % 2]
                ptb = i % NPT
                u["sb"] = sb_
                u["ptb"] = ptb
                mm = u["mm"]
                k = 0
                while k < len(mm):
                    (kap, qap, col, nq, vt, isprev, pieces) = mm[k]
                    if k + 1 < len(mm) and mm[k + 1][4] == vt and (not isprev) and mm[k + 1][5] and u["g"] < 2:
                        q2 = bass.AP(qap.tensor, qap.offset, [qap.ap[0], [1, 2 * nq]])
                        S.op("pe", lambda e, kap=kap, q2=q2, col=col, nq=nq, sb_=sb_: e.matmul(PSF[sb_][:, col:col + 2 * nq], lhsT=kap, rhs=q2, start=True, stop=True),
                             reads=BKT + BQT, writes=[BPS[sb_]])
                        k += 2
                        continue
                    S.op("pe", lambda e, kap=kap, qap=qap, col=col, nq=nq, sb_=sb_: e.matmul(PSF[sb_][:, col:col + nq], lhsT=kap, rhs=qap, start=True, stop=True),
                         reads=BKT + BQT, writes=[BPS[sb_]])
                    k += 1
                col = u["ncol"]
                pat = u["pat"]
                S.op("act", lambda e, col=col, ptb=ptb, sb_=sb_: e.activation(out=PT[ptb][:, 0:col], in_=PSF[sb_][:, 0:col], func=AF.Exp, scale=0.125),
                     reads=[BPS[sb_]], writes=[BPT[ptb]])
                assert col == 512
                mview, pview = pat_views(pat, PT[ptb])
                S.op("dve", lambda e, mview=mview, pview=pview: e.tensor_tensor(out=pview, in0=pview, in1=mview, op=ALU.mult),
                     reads=[BPT[ptb], BPAT], writes=[BPT[ptb]])

            def emit_pv(u):
                ob = (4, 5)
                h = u["h"]
                half = u["half"]
                ptb = u["ptb"]
                for (kap, qap, c0, nq, vt, isprev, pieces) in u["mm"]:
                    for (bank, pc0, pstr, pn, roff) in pieces:
                        oap = PSF[ob[bank]][:, sl(pc0, pn, pstr)]
                        S.op("pe", lambda e, oap=oap, vt=vt, ptb=ptb, c0=c0, roff=roff, pn=pn, isprev=isprev, h=h: e.matmul(
                            oap, lhsT=VAUG[:, vt, h, :], rhs=PT[ptb][:, c0 + roff:c0 + roff + pn], start=isprev, stop=(not isprev)),
                            reads=[BVA, BPT[ptb]], writes=[BPS[ob[bank]]])
                if u["last"]:
                    for bank in range(2):
                        dst = OACC[h][:, half * 1024 + bank * 512: half * 1024 + (bank + 1) * 512]
                        if u["g"] == 0:
                            S.op("act", lambda e, dst=dst, bank=bank: e.copy(out=dst, in_=PSF[ob[bank]][:, :]), reads=[BPS[ob[bank]]], writes=[BOACC[h][half]])
                        else:
                            ncls = 2 if u["g"] == 1 else 8
                            dd = DIL[u["g"]]
                            o0 = OACC[h][:, half * 1024 + ncls * bank:half * 1024 + ncls * bank + 1]
                            dview = bass.AP(o0.tensor, o0.offset, [o0.ap[0], [1, ncls], [dd, 512 // ncls]])
                            sview = PSF[ob[bank]][:, :].rearrange("p (r i) -> p r i", r=ncls)
                            S.op("dve", lambda e, dview=dview, sview=sview: e.tensor_tensor(out=dview, in0=dview, in1=sview, op=ALU.add),
                                 reads=[BPS[ob[bank]], BOACC[h][half]], writes=[BOACC[h][half]])

            def attention_it(it):
                U = attention_units(it, 0) + attention_units(it, 1)
                emit_st(U[0], 0)
                for i in range(len(U)):
                    if i + 1 < len(U):
                        emit_st(U[i + 1], i + 1)
                    emit_pv(U[i])

            def normalize(hp):
                for h in range(2):
                    for qq in range(4):
                        cs_ = slice(qq * 512, (qq + 1) * 512)
                        S.op("act", lambda e, h=h, cs_=cs_: e.activation(out=PSF[6][64:128, :], in_=OACC[h][64:128, cs_], func=AF.Ln), reads=BOACC[h], writes=[BPS[6]])
                        S.op("act", lambda e: e.activation(out=PSF[6][64:128, :], in_=PSF[6][64:128, :], func=AF.Exp, scale=-1.0), reads=[BPS[6]], writes=[BPS[6]])
                        S.op("dve", lambda e, h=h, cs_=cs_: e.tensor_tensor(out=OT[h * 64:(h + 1) * 64, hp, cs_], in0=OACC[h][0:64, cs_], in1=PSF[6][64:128, :], op=ALU.mult),
                             reads=BOACC[h] + [BPS[6]], writes=[BOT])

            import os
            NIT = int(os.environ.get("KDBG_NIT", "24"))
            DBG = os.environ.get("KDBG", "")
            if NIT < 24:
                S.op("pool", lambda e: e.memset(OT, 0.0), writes=[BOT])
                for h in range(2):
                    S.op("pool", lambda e, h=h: e.memset(OACC[h], 1.0), writes=BOACC[h])
            if NIT > 0:
                load_w(0)
                load_w(1)
            for it in range(NIT):
                hp, g = divmod(it, 3)
                for wh in range(3):
                    if "noproj%d" % wh in DBG:
                        continue
                    project(it, wh)
                if it + 2 < NIT:
                    load_w(it + 2)
                if "novaug" not in DBG:
                    build_vaug(g)
                if "noatt" not in DBG:
                    attention_it(it)
                if g == 2 and "nonorm" not in DBG:
                    normalize(hp)
            S.barrier()
            al.pos = mark_ot
            WOB = al.bf(8 * 1024).rearrange("p (a b) -> p a b", a=8)
            BWOB = [Buf() for _ in range(8)]
            for kc in range(8):
                S.dma("pool", f"c_wo{kc % 4}", lambda e, kc=kc: e.dma_start(out=WOB[:, kc, :], in_=t_wout.ap()[kc * 128:(kc + 1) * 128, :]), writes=[BWOB[kc]])
            alloc_ln(0)
            for t in range(16):
                S.dma("sp", f"xres{t % 4}", lambda e, t=t: e.dma_start(out=XTOK[:, t, :], in_=t_xo.ap()[t * 128:(t + 1) * 128, :]), writes=[BXTOK[t]])
            emit_outproj_residual(OT, BOT, WOB, BWOB, 8)
            emit_ln_all(wide=True)
            S.barrier()
            al.pos = mark


        def emit_load_x():
            mark = al.pos
            XIN = al.bf(1024)
            BXIN = Buf()
            for t in range(16):
                S.dma("sp", f"xres{t % 4}", lambda e, t=t: e.dma_start(out=XTOK[:, t, :], in_=t_xo.ap()[t * 128:(t + 1) * 128, :]), writes=[BXTOK[t]])
            for t in range(16):
                S.op("act", lambda e, t=t: e.copy(out=XIN, in_=XTOK[:, t, :]), reads=[BXTOK[t]], writes=[BXIN])
                pb = 6 + (t % 2)
                for kc in range(8):
                    S.op("pe", lambda e, kc=kc, pb=pb: e.transpose(out=PSB[pb][:, kc * 128:(kc + 1) * 128], in_=XIN[:, kc * 128:(kc + 1) * 128], identity=IDENT),
                         reads=[BXIN, BID], writes=[BPS[pb]])
                dst = XT[:, :, t * 128:(t + 1) * 128]
                src = PSB[pb][:, 0:1024].rearrange("p (a b) -> p a b", a=8)
                S.op("dve", lambda e, dst=dst, src=src: e.tensor_copy(out=dst, in_=src), reads=[BPS[pb]], writes=[BXT[t // 4]])
            S.barrier()
            al.pos = mark

        def emit_hgrn(final_scan_only=False, s0_mode="dram"):
            so = final_scan_only
            S.cur_delta = 0.0
            mark = al.pos
            ON = al.bf(8 * NT).rearrange("p (a b) -> p a b", a=8)
            BON = Buf()
            mark_on = al.pos
            QD = al.bf(NT)
            KD = al.bf(NT)
            BQD = [Buf() for _ in range(4)]
            BKD = [Buf() for _ in range(4)]
            VTOK = [al.bf(16 * 128).rearrange("p (a b) -> p a b", a=16) for _ in range(2)]
            BVTOK = [[Buf() for _ in range(4)] for _ in range(2)]
            KLT = [al.bf(16 * 128).rearrange("p (a b) -> p a b", a=16) for _ in range(2)]
            BKLT = [[Buf() for _ in range(4)] for _ in range(2)]
            CD = [al.f32(32) for _ in range(2)]
            BCD = [[Buf() for _ in range(4)] for _ in range(2)]
            SALL = al.bf(33 * 128).rearrange("p (a b) -> p a b", a=33)
            BSALL = Buf()
            SF = [al.f32(128) for _ in range(2)]
            BSF = [Buf(), Buf()]
            WH = [[al.bf(8 * 128).rearrange("p (a b) -> p a b", a=8) for _ in range(3)] for _ in range(2)]
            BWH = [[Buf() for _ in range(3)] for _ in range(2)]
            T0 = [al.f32(512) for _ in range(3)]
            BT0 = [Buf() for _ in range(3)]
            T1 = [al.f32(512) for _ in range(2)]
            BT1 = [Buf() for _ in range(2)]
            T3 = [al.f32(512) for _ in range(2)]
            BT3 = [Buf() for _ in range(2)]
            if not so:
                T4 = [al.f32(512) for _ in range(2)]
                BT4 = [Buf() for _ in range(2)]
                _t5 = al.f32(512)
                T5 = [_t5, _t5]
                _bt5 = Buf()
                BT5 = [_bt5, _bt5]
            VTB = [al.bf(512) for _ in range(2)]
            BVTB = [Buf() for _ in range(2)]
            KVB = [al.f32(128) for _ in range(4)]
            BKVB = [Buf() for _ in range(4)]
            AT = al.bf(512)
            BAT = Buf()
            OS = al.f32(512)
            BOS = Buf()
            OSQ = al.bf(512)
            BOSQ = Buf()
            RS = al.f32(512)
            BRS = Buf()
            RST = al.f32(512)
            TRI = al.f32(128)
            ONES = al.bf(128)
            LBS = al.f32(40)
            BC2 = Buf()
            S.dma("sp", "c_rst", lambda e: e.dma_start(out=RST, in_=t_rst.ap()), writes=[BC2])
            S.dma("sp", "c_tri", lambda e: e.dma_start(out=TRI, in_=t_tri.ap()), writes=[BC2])
            S.dma("sp", "c_lbl", lambda e: e.dma_start(out=LBS[:, 0:16], in_=t_lbl.ap()), writes=[BC2])
            S.dma("sp", "c_hng", lambda e: e.dma_start(out=LBS[:, 32:40], in_=t_hng.ap()), writes=[BC2])
            FLG = al.f32(2)
            if mode != "l1":
                S.dma("sp", "c_flg", lambda e: e.dma_start(out=FLG, in_=t_flags.ap()), writes=[BC2])
            S.op("pool", lambda e: e.memset(ONES, 1.0), writes=[BC2])
            S.op("dve", lambda e: e.tensor_tensor(out=LBS[:, 16:24], in0=LBS[:, 8:16], in1=LBS[:, 0:8], op=ALU.subtract), reads=[BC2], writes=[BC2])
            S.op("act", lambda e: e.activation(out=LBS[:, 16:24], in_=LBS[:, 16:24], func=AF.Sigmoid), reads=[BC2], writes=[BC2])
            S.op("dve", lambda e: e.tensor_scalar(out=LBS[:, 24:32], in0=LBS[:, 16:24], scalar1=-1.0, scalar2=1.0, op0=ALU.mult, op1=ALU.add), reads=[BC2], writes=[BC2])

            def load_w(h):
                s_ = h % 2
                for wh in range(3):
                    if so and wh == 0:
                        continue
                    c0 = wh * 1024 + h * 128
                    src = t_hwin.ap()[:, c0:c0 + 128].rearrange("(kc p) c -> p kc c", p=128)
                    S.dma("pool", f"wh{s_}{wh}", lambda e, s_=s_, wh=wh, src=src: e.dma_start(out=WH[s_][wh], in_=src), writes=[BWH[s_][wh]])

            cnt = {"pj": 0}

            PJB = [0, 1, 2, 6, 7] if so else [0, 1, 5]

            def proj(h, tb, wh):
                s_ = h % 2
                pb = PJB[cnt["pj"] % len(PJB)]
                cnt["pj"] += 1
                cols = slice(tb * 512, (tb + 1) * 512)
                for kc in range(8):
                    S.op("pe", lambda e, kc=kc, pb=pb, cols=cols: e.matmul(PSF[pb][:, :], lhsT=WH[s_][wh][:, kc, :], rhs=XT[:, kc, cols], start=(kc == 0), stop=(kc == 7)),
                         reads=[BWH[s_][wh], BXT[tb]], writes=[BPS[pb]])
                return pb

            def st_A(j):
                h, tb = divmod(j, 4)
                pz = proj(h, tb, 1)
                S.op("act", lambda e: e.activation(out=T0[j % 3], in_=PSF[pz][:, :], func=AF.Sigmoid), reads=[BPS[pz]], writes=[BT0[j % 3]])
                pv = proj(h, tb, 2)
                S.op("dve", lambda e: e.tensor_copy(out=VTB[j % 2], in_=PSF[pv][:, :]), reads=[BPS[pv]], writes=[BVTB[j % 2]])
                lb = LBS[:, 16 + h:17 + h]
                oml = LBS[:, 24 + h:25 + h]
                S.op("dve", lambda e: e.tensor_scalar(out=T0[j % 3], in0=T0[j % 3], scalar1=oml, scalar2=lb, op0=ALU.mult, op1=ALU.add), reads=[BT0[j % 3], BC2], writes=[BT0[j % 3]])

            def st_Hv(j):
                h, tb = divmod(j, 4)
                for k in range(4):
                    S.op("pe", lambda e, k=k: e.transpose(out=PSB[3][:, k * 128:(k + 1) * 128], in_=VTB[j % 2][:, k * 128:(k + 1) * 128], identity=IDENT),
                         reads=[BVTB[j % 2], BID], writes=[BPS[3]])
                S.op("dve", lambda e: e.tensor_copy(out=VTOK[h % 2][:, tb * 4:(tb + 1) * 4, :], in_=PSB[3][:, 0:512].rearrange("p (a b) -> p a b", a=4)), reads=[BPS[3]], writes=[BVTOK[h % 2][tb]])

            def st_D(j):
                S.op("act", lambda e: e.activation(out=T1[j % 2], in_=T0[j % 3], func=AF.Ln), reads=[BT0[j % 3]], writes=[BT1[j % 2]])
                S.op("dve", lambda e: e.tensor_tensor_scan(out=T1[j % 2], data0=RST, data1=T1[j % 2], initial=0.0, op0=ALU.mult, op1=ALU.add),
                     reads=[BT1[j % 2], BC2], writes=[BT1[j % 2]])

            def st_F(j):
                h, tb = divmod(j, 4)
                S.op("act", lambda e: e.activation(out=CD[h % 2][:, tb * 8:(tb + 1) * 8], in_=T1[j % 2][:, 63:512:64], func=AF.Exp), reads=[BT1[j % 2]], writes=[BCD[h % 2][tb]])
                S.op("act", lambda e: e.activation(out=T3[j % 2], in_=T1[j % 2], func=AF.Exp, scale=-1.0), reads=[BT1[j % 2]], writes=[BT3[j % 2]])
                if not so:
                    S.op("act", lambda e: e.activation(out=T4[j % 2], in_=T1[j % 2], func=AF.Exp), reads=[BT1[j % 2]], writes=[BT4[j % 2]])

            def st_Aq(j):
                h, tb = divmod(j, 4)
                pq = proj(h, tb, 0)
                S.op("act", lambda e: e.activation(out=T5[j % 2], in_=PSF[pq][:, :], func=AF.Sigmoid), reads=[BPS[pq]], writes=[BT5[j % 2]])
                S.op("dve", lambda e: e.tensor_tensor(out=T5[j % 2], in0=PSF[pq][:, :], in1=T5[j % 2], op=ALU.mult), reads=[BPS[pq], BT5[j % 2]], writes=[BT5[j % 2]])

            def st_G(j):
                h, tb = divmod(j, 4)
                cols = slice(tb * 512, (tb + 1) * 512)
                S.op("dve", lambda e: e.tensor_tensor(out=T0[j % 3], in0=T0[j % 3], in1=T3[j % 2], op=ALU.mult), reads=[BT0[j % 3], BT3[j % 2]], writes=[BT0[j % 3]])
                S.op("dve", lambda e: e.tensor_tensor(out=T3[j % 2], in0=T3[j % 2], in1=T0[j % 3], op=ALU.subtract), reads=[BT0[j % 3], BT3[j % 2]], writes=[BT3[j % 2]])
                S.op("pool", lambda e: e.tensor_copy(out=KD[:, cols], in_=T3[j % 2]), reads=[BT3[j % 2]], writes=[BKD[tb]])
                if not so:
                    S.op("dve", lambda e: e.tensor_tensor(out=QD[:, cols], in0=T5[j % 2], in1=T4[j % 2], op=ALU.mult), reads=[BT5[j % 2], BT4[j % 2]], writes=[BQD[tb]])

            def st_Hk(j):
                h, tb = divmod(j, 4)
                for k in range(4):
                    S.op("pe", lambda e, k=k: e.transpose(out=PSB[3][:, k * 128:(k + 1) * 128], in_=KD[:, tb * 512 + k * 128:tb * 512 + (k + 1) * 128], identity=IDENT),
                         reads=[BKD[tb], BID], writes=[BPS[3]])
                S.op("dve", lambda e: e.tensor_copy(out=KLT[h % 2][:, tb * 4:(tb + 1) * 4, :], in_=PSB[3][:, 0:512].rearrange("p (a b) -> p a b", a=4)), reads=[BPS[3]], writes=[BKLT[h % 2][tb]])

            def head_tail(h):
                hp_ = h % 2
                if s0_mode == "dram":
                    S.dma("pool", "s0b", lambda e: e.dma_start(out=SALL[:, 0, :], in_=t_s0.ap()[h]), writes=[BSALL])
                    S.dma("sp", "s0f", lambda e: e.dma_start(out=SF[0], in_=t_s0.ap()[h]), writes=[BSF[0]])
                elif s0_mode == "zero":
                    S.op("pool", lambda e: e.memset(SF[0], 0.0), writes=[BSF[0]])
                else:
                    S.dma("sp", "s0f", lambda e: e.dma_start(out=SF[0], in_=t_bout.ap()[h * 128:(h + 1) * 128, :]), reads=[BBOUT], writes=[BSF[0]])
                    S.op("dve", lambda e: e.tensor_scalar(out=SF[0], in0=SF[0], scalar1=FLG[:, 1:2], scalar2=None, op0=ALU.mult), reads=[BSF[0], BC2], writes=[BSF[0]])
                    S.op("act", lambda e: e.copy(out=SALL[:, 0, :], in_=SF[0]), reads=[BSF[0]], writes=[BSALL])

            def scan_piece(h, kq):
                hp_ = h % 2
                for c in range(8 * kq, 8 * kq + 8):
                    t, hc = divmod(c, 2)
                    rows = slice(hc * 64, (hc + 1) * 64)
                    kb = (4 + (c // 4) % 2) if so else 4
                    kc_ = c % 4
                    S.op("pe", lambda e, t=t, rows=rows, kb=kb, kc_=kc_: e.matmul(PSF[kb][:, kc_ * 128:(kc_ + 1) * 128], lhsT=KLT[hp_][rows, t, :], rhs=VTOK[hp_][rows, t, :], start=True, stop=True),
                         reads=BKLT[hp_] + BVTOK[hp_], writes=[BPS[kb]])
                    kv = KVB[c % 4]
                    S.op("act", lambda e, kv=kv, c=c, kb=kb, kc_=kc_: e.activation(out=kv, in_=PSF[kb][:, kc_ * 128:(kc_ + 1) * 128], func=AF.Identity, scale=CD[hp_][:, c:c + 1]),
                         reads=[BPS[kb]] + BCD[hp_], writes=[BKVB[c % 4]])
                    so_, sn_ = SF[c % 2], SF[(c + 1) % 2]
                    S.op("dve", lambda e, so_=so_, sn_=sn_, c=c, kv=kv: e.scalar_tensor_tensor(out=sn_, in0=so_, scalar=CD[hp_][:, c:c + 1], in1=kv, op0=ALU.mult, op1=ALU.add),
                         reads=[BSF[c % 2], BKVB[c % 4]] + BCD[hp_], writes=[BSF[(c + 1) % 2]])
                    if not so:
                        S.op("pool", lambda e, sn_=sn_, c=c: e.tensor_copy(out=SALL[:, c + 1, :], in_=sn_), reads=[BSF[(c + 1) % 2]], writes=[BSALL])
                if kq < 3:
                    return
                if mode == "l1":
                    S.dma("sp", "sout", lambda e: e.dma_start(out=t_sout.ap()[h], in_=SF[0]), reads=[BSF[0]])
                if so:
                    S.op("act", lambda e: e.activation(out=OS[:, 0:128], in_=SF[0], func=AF.Identity, scale=FLG[:, 0:1]), reads=[BSF[0], BC2], writes=[BOS])
                    S.dma("sp", "bin", lambda e: e.dma_start(out=t_bin.ap()[h * 128:(h + 1) * 128, :], in_=OS[:, 0:128]), reads=[BOS], writes=[BBIN])

            def out_block(h, tb):
                hp_ = h % 2
                if True:
                    cols = slice(tb * 512, (tb + 1) * 512)
                    for k in range(4):
                        t = tb * 4 + k
                        tc_ = slice(t * 128, (t + 1) * 128)
                        S.op("pe", lambda e, k=k, tc_=tc_: e.matmul(PSF[2][:, k * 128:(k + 1) * 128], lhsT=KD[:, tc_], rhs=QD[:, tc_], start=True, stop=True),
                             reads=[BKD[tb], BQD[tb]], writes=[BPS[2]])
                    S.op("dve", lambda e: e.tensor_tensor(out=AT.rearrange("p (a b) -> p a b", a=4), in0=PSF[2][:, :].rearrange("p (a b) -> p a b", a=4),
                                                          in1=bass.AP(TRI.tensor, TRI.offset, [TRI.ap[0], [0, 4], [1, 128]]), op=ALU.mult),
                         reads=[BPS[2], BC2], writes=[BAT])
                    for k in range(4):
                        t = tb * 4 + k
                        S.op("pe", lambda e, k=k, t=t: e.matmul(PSF[6][:, k * 128:(k + 1) * 128], lhsT=VTOK[hp_][:, t, :], rhs=AT[:, k * 128:(k + 1) * 128], start=True, stop=False),
                             reads=BVTOK[hp_] + [BAT], writes=[BPS[6]])
                        for hc in range(2):
                            c = 2 * t + hc
                            S.op("pe", lambda e, k=k, hc=hc, c=c: e.matmul(PSF[6][:, k * 128 + hc * 64:k * 128 + (hc + 1) * 64], lhsT=SALL[:, c, :],
                                                                           rhs=QD[:, c * 64:(c + 1) * 64], start=False, stop=(hc == 1)),
                                 reads=[BSALL, BQD[tb]], writes=[BPS[6]])
                    S.op("act", lambda e: e.activation(out=OSQ, in_=PSF[6][:, :], func=AF.Square), reads=[BPS[6]], writes=[BOSQ])
                    S.op("pe", lambda e: e.matmul(PSF[7][:, :], lhsT=ONES, rhs=OSQ, start=True, stop=True), reads=[BOSQ, BC2], writes=[BPS[7]])
                    S.op("act", lambda e: e.activation(out=RS, in_=PSF[7][:, :], func=AF.Ln, bias=EPS_T[:, 1:2], scale=1.0 / 128.0), reads=[BPS[7], BCONST], writes=[BRS])
                    S.op("act", lambda e: e.activation(out=RS, in_=RS, func=AF.Exp, scale=-0.5), reads=[BRS], writes=[BRS])
                    S.op("dve", lambda e, cols=cols: e.scalar_tensor_tensor(out=ON[:, h, cols], in0=PSF[6][:, :], scalar=LBS[:, 32 + h:33 + h], in1=RS, op0=ALU.mult, op1=ALU.mult),
                         reads=[BPS[6], BRS, BC2], writes=[BON])

            NB = 32
            load_w(0)
            load_w(1)
            for tau in range(NB + 6):
                if 0 <= tau - 3 < NB:
                    st_Hk(tau - 3)
                    if (tau - 3) % 4 == 3:
                        head_tail((tau - 3) // 4)
                jj = tau - 6
                if 0 <= jj < NB:
                    scan_piece(jj // 4, jj % 4)
                    if not so:
                        out_block(jj // 4, jj % 4)
                if 0 <= tau - 2 < NB:
                    st_F(tau - 2)
                if 0 <= tau - 1 < NB:
                    st_D(tau - 1)
                if 0 <= tau - 2 < NB:
                    if not so:
                        st_Aq(tau - 2)
                    st_G(tau - 2)
                    if (tau - 2) % 4 == 3 and (tau - 2) // 4 + 2 < 8:
                        load_w((tau - 2) // 4 + 2)
                if 0 <= tau - 1 < NB:
                    st_Hv(tau - 1)
                if tau < NB:
                    st_A(tau)
            S.barrier()
            if so:
                S.dma("pool", "cc", lambda e: e.collective_compute("AllReduce", ALU.add, replica_groups=[[0, 1], [2, 3], [4, 5], [6, 7]],
                                                                 ins=[t_bin.ap().opt()], outs=[t_bout.ap().opt()]), reads=[BBIN], writes=[BBOUT], inc=1)
                al.pos = mark
                return
            al.pos = mark_on
            WOB = al.bf(8 * 1024).rearrange("p (a b) -> p a b", a=8)
            BWOB = [Buf() for _ in range(8)]
            for kc in range(8):
                S.dma("pool", f"c_hwo{kc % 4}", lambda e, kc=kc: e.dma_start(out=WOB[:, kc, :], in_=t_hwout.ap()[kc * 128:(kc + 1) * 128, :]), writes=[BWOB[kc]])
            alloc_ln(2)
            emit_outproj_residual(ON, BON, WOB, BWOB, 8)
            emit_ln_all(wide=True)
            S.barrier()
            al.pos = mark

        if do_l0:
            emit_attention()
            if stage >= 2:
                emit_ffn(0, final=(mode == "l0"))
        BBIN = Buf()
        BBOUT = Buf()
        if do_l1:
            if mode == "l1":
                emit_load_x()
                emit_hgrn()
            else:
                emit_hgrn(final_scan_only=True, s0_mode="zero")
                emit_hgrn(s0_mode="bounce")
            emit_ffn(1, final=True)
        if mode == "l0" and stage < 2:
            for t in range(16):
                S.dma("sp", f"o_out{t % 4}", lambda e, t=t: e.dma_start(out=t_out.ap()[t * 128:(t + 1) * 128, :], in_=XTOK[:, t, :]), reads=[BXTOK[t]])
        S.wait_all_dma("sp", [f"o_out{i}" for i in range(4)] + (["sout"] if mode == "l1" else []))
        S.emit(st)
    return nc


def _tables(half):
    pos = np.arange(4096, dtype=np.float32) + np.float32(half * 2048 - 2048)
    inv = (np.float32(10000.0) ** (-np.arange(32, dtype=np.float32) * np.float32(2.0 / 64))).astype(np.float32)
    ang = pos[None, :].astype(np.float32) * inv[:, None]
    c = np.cos(ang).astype(np.float32)
    s = np.sin(ang).astype(np.float32)
    cs = np.concatenate([c, c, c, c], axis=0)
    sn = np.concatenate([-s, s, -s, s], axis=0)
    return np.ascontiguousarray(cs), np.ascontiguousarray(sn)


def _patterns(half):
    k = np.arange(128)[:, None]
    q = np.arange(128)[None, :]
    mprev = (k >= q).astype(np.float32)
    mcur = (k <= q).astype(np.float32)
    mph = mprev * np.float32(1.0 if half == 1 else 0.0)
    return np.ascontiguousarray(np.concatenate([mph, mcur, mprev, mcur], axis=1))


_CACHE = {}


def _get_nc(stage, mode):
    key = (stage, mode)
    if key not in _CACHE:
        _CACHE[key] = build(stage, mode)
    return _CACHE[key]


def _common_maps(ln_mix_g, ln_mix_b, ln_ffn_g, ln_ffn_b, ffn_w_up, ffn_w_down):
    lng = np.ascontiguousarray(np.stack([ln_mix_g[0], ln_ffn_g[0], ln_mix_g[1], ln_ffn_g[1]], axis=0), dtype=np.float32)
    lnb = np.ascontiguousarray(np.stack([ln_mix_b[0], ln_ffn_b[0], ln_mix_b[1], ln_ffn_b[1]], axis=0), dtype=np.float32)
    return {"ident": np.eye(128, dtype=np.float32), "lng": lng, "lnb": lnb,
            "wup": np.ascontiguousarray(ffn_w_up, dtype=np.float32), "wdn": np.ascontiguousarray(ffn_w_down, dtype=np.float32)}


def _l0_maps(x, attn_w_in, attn_w_out):
    perm = np.zeros((128, 128), np.float32)
    for m in range(128):
        perm[m ^ 32, m] = 1.0
    awin = np.ascontiguousarray(attn_w_in[0], dtype=np.float32)
    awout = np.ascontiguousarray(attn_w_out[0], dtype=np.float32)
    maps = []
    for c in range(8):
        b, half = divmod(c, 2)
        xh = np.ascontiguousarray(x[b, 0:NT]) if half == 1 else np.zeros((NT, D), np.float32)
        cs, sn = _tables(half)
        maps.append({"xh": xh, "cs": cs, "sn": sn, "pats": _patterns(half), "perm": perm, "awin": awin, "awout": awout})
    return maps


def _l1_maps(hgrn_w_in, hgrn_w_out, hgrn_norm_g, lb_logits):
    hng = np.ascontiguousarray(hgrn_norm_g[0].reshape(8, 128).T, dtype=np.float32)
    lbl = np.ascontiguousarray(lb_logits.reshape(2, 8, 128).transpose(2, 0, 1).reshape(128, 16), dtype=np.float32)
    j = np.arange(128)[:, None]
    i = np.arange(128)[None, :]
    tri = ((j // 64 == i // 64) & (j <= i)).astype(np.float32)
    rst = np.ones((128, 512), np.float32)
    rst[:, ::64] = 0.0
    return {"hwin": np.ascontiguousarray(hgrn_w_in[0], dtype=np.float32), "hwout": np.ascontiguousarray(hgrn_w_out[0], dtype=np.float32),
            "hng": hng, "lbl": lbl, "tri": tri, "rst": rst}


def _run(nc, in_maps):
    import os
    ncores = int(os.environ.get("KDBG_NCORES", "8"))
    res = run_bass_kernel_spmd(nc, in_maps[:ncores], core_ids=list(range(ncores)))
    return res.results


def kernel(x, attn_w_in, attn_w_out, hgrn_w_in, hgrn_w_out, hgrn_norm_g, lb_logits,
           ln_mix_g, ln_mix_b, ln_ffn_g, ln_ffn_b, ffn_w_up, ffn_w_down, _stage=99, _mode="full", _x1=None):
    x = np.ascontiguousarray(x, dtype=np.float32)
    B = x.shape[0]
    common = _common_maps(ln_mix_g, ln_mix_b, ln_ffn_g, ln_ffn_b, ffn_w_up, ffn_w_down)
    if _mode == "full":
        l0 = _l0_maps(x, attn_w_in, attn_w_out)
        l1 = _l1_maps(hgrn_w_in, hgrn_w_out, hgrn_norm_g, lb_logits)
        in_maps = []
        for c in range(8):
            b, half = divmod(c, 2)
            m = dict(common)
            m.update(l0[c])
            m.update(l1)
            m["xo"] = np.ascontiguousarray(x[b, half * NT:(half + 1) * NT])
            fl = np.zeros((128, 2), np.float32)
            fl[:, 0] = 1.0 if half == 0 else 0.0
            fl[:, 1] = 1.0 if half == 1 else 0.0
            m["flags"] = fl
            in_maps.append(m)
        r = _run(_get_nc(99, "full"), in_maps)
        out = np.zeros((B, 2 * NT, D), np.float32)
        for c in range(8):
            b, half = divmod(c, 2)
            out[b, half * NT:(half + 1) * NT] = r[c]["out"]
        return out
    if _mode in ("l0", "chain"):
        l0 = _l0_maps(x, attn_w_in, attn_w_out)
        in_maps = []
        for c in range(8):
            b, half = divmod(c, 2)
            m = dict(common)
            m.update(l0[c])
            m["xo"] = np.ascontiguousarray(x[b, half * NT:(half + 1) * NT])
            in_maps.append(m)
        r0 = _run(_get_nc(_stage, "l0"), in_maps)
        x1 = [r0[c]["out"] for c in range(len(r0))]
        if _mode == "l0":
            return np.concatenate(x1, axis=0).reshape(-1, 2 * NT, D) if len(x1) == 8 else np.concatenate(x1, axis=0)
    else:
        x1 = [np.ascontiguousarray(_x1[c // 2, (c % 2) * NT:(c % 2 + 1) * NT]) for c in range(8)]
    l1 = _l1_maps(hgrn_w_in, hgrn_w_out, hgrn_norm_g, lb_logits)
    nc1 = _get_nc(99, "l1")
    zeros = np.zeros((8, 128, 128), np.float32)
    in_maps = []
    for c in range(len(x1)):
        m = dict(common)
        m.update(l1)
        m["xo"] = x1[c]
        m["s0"] = zeros
        in_maps.append(m)
    ra = _run(nc1, in_maps)
    for c in range(len(x1)):
        if c % 2 == 1:
            in_maps[c]["s0"] = np.ascontiguousarray(ra[c - 1]["sout"])
    rb = _run(nc1, in_maps)
    outs = [rb[c]["out"] for c in range(len(rb))]
    if len(outs) < 8:
        return np.concatenate(outs, axis=0)
    out = np.zeros((B, 2 * NT, D), np.float32)
    for c in range(8):
        b, half = divmod(c, 2)
        out[b, half * NT:(half + 1) * NT] = outs[c]
    return out
```
